# Optimizing a Trainium2 kernel written in Bass

```python
import math
import jax, jax.numpy as jnp
from jax import lax
import numpy as np

D_MODEL = 2048
BATCH = 4
SEQ = 2048
DEPTH = 4
DEC_BATCH = 128
DEC_SEQ = 4
PAST_LEN = 16384
PAGE_SIZE = 128

D_GDN = 3 * D_MODEL // 8
D_SSD = 3 * D_MODEL // 8
D_POOL = D_MODEL - D_GDN - D_SSD
D_MIX = D_GDN + D_SSD + D_POOL

CONV_K = 4
GDN_HEAD_DIM = 128
GDN_HEADS = D_GDN // GDN_HEAD_DIM
GDN_CHUNK = 64

SSD_HEAD_DIM = 64
SSD_HEADS = D_SSD // SSD_HEAD_DIM
SSD_STATE = 128
SSD_GROUPS = 2
SSD_HEADS_PER_GROUP = SSD_HEADS // SSD_GROUPS
SSD_CHUNK = 128
SSD_CONV_CH = D_SSD + 2 * SSD_GROUPS * SSD_STATE

POOL_WINDOWS = (2, 4, 8, 16)
POOL_GROUPS = len(POOL_WINDOWS)
POOL_GROUP_DIM = D_POOL // POOL_GROUPS
POOL_BUF = max(POOL_WINDOWS) - 1

IN_SIZES = (3 * D_GDN, D_GDN, GDN_HEADS, GDN_HEADS, D_SSD, SSD_CONV_CH, SSD_HEADS, D_POOL, D_POOL)
D_IN_PROJ = sum(IN_SIZES)
EPS = 1e-6

kernel_name = 'hybrid_gdn_ssd_pool_decode_step'


def _rmsnorm(x, w):
    xf = x.astype(jnp.float32)
    y = xf * lax.rsqrt(jnp.mean(xf * xf, axis=-1, keepdims=True) + EPS)
    return (y * w.astype(jnp.float32)).astype(x.dtype)


def _l2norm(x):
    xf = x.astype(jnp.float32)
    return xf * lax.rsqrt(jnp.sum(xf * xf, axis=-1, keepdims=True) + EPS)


def _pad_time(a, n_pad):
    if n_pad == 0:
        return a
    pad = [(0, 0)] * a.ndim
    pad[1] = (0, n_pad)
    return jnp.pad(a, pad)


def _causal_conv(u, buf, w, b):
    t_len = u.shape[1]
    cat = jnp.concatenate([buf.astype(u.dtype), u], axis=1)
    out = sum(w[j] * cat[:, j:j + t_len] for j in range(CONV_K))
    if b is not None:
        out = out + b
    return out, cat[:, t_len:]


def _gated_delta_chunked(q, k, v, beta, g, s0):
    bsz, t_len, n_h, dv = v.shape
    c = min(GDN_CHUNK, t_len)
    n_pad = (-t_len) % c
    q, k, v, beta, g = [_pad_time(a.astype(jnp.float32), n_pad) for a in (q, k, v, beta, g)]
    n = (t_len + n_pad) // c

    def chunks(a):
        a = a.reshape((bsz, n, c, n_h) + a.shape[3:])
        return jnp.moveaxis(a, (1, 3), (0, 2))

    qc, kc, vc, bc, gc = chunks(q), chunks(k), chunks(v), chunks(beta), chunks(g)
    gcum = jnp.cumsum(gc, axis=-1)
    causal = jnp.tril(jnp.ones((c, c), bool))
    strict = jnp.tril(jnp.ones((c, c), bool), -1)
    diff = gcum[..., :, None] - gcum[..., None, :]
    decay = jnp.where(causal, jnp.exp(jnp.where(causal, diff, 0.0)), 0.0)
    kk = jnp.einsum('nbhid,nbhjd->nbhij', kc, kc)
    a_mat = jnp.where(strict, kk * decay * bc[..., :, None], 0.0)
    lhs = jnp.eye(c, dtype=jnp.float32) + a_mat
    rhs = jnp.concatenate([vc * bc[..., None], kc * (bc * jnp.exp(gcum))[..., None]], axis=-1)
    sol = lax.linalg.triangular_solve(lhs, rhs, left_side=True, lower=True, unit_diagonal=True)
    u_c, w_c = sol[..., :dv], sol[..., dv:]
    qk = jnp.where(causal, jnp.einsum('nbhid,nbhjd->nbhij', qc, kc) * decay, 0.0)
    q_dec = qc * jnp.exp(gcum)[..., None]
    k_dec = kc * jnp.exp(gcum[..., -1:] - gcum)[..., None]
    g_last = jnp.exp(gcum[..., -1])

    def step(s, inp):
        u_i, w_i, qk_i, q_i, k_i, gl_i = inp
        v_new = u_i - jnp.einsum('bhcd,bhde->bhce', w_i, s)
        o = jnp.einsum('bhcd,bhde->bhce', q_i, s) + jnp.einsum('bhij,bhje->bhie', qk_i, v_new)
        s = s * gl_i[..., None, None] + jnp.einsum('bhcd,bhce->bhde', k_i, v_new)
        return s, o

    s, o = lax.scan(step, s0.astype(jnp.float32), (u_c, w_c, qk, q_dec, k_dec, g_last))
    o = jnp.moveaxis(o, (0, 2), (1, 3)).reshape(bsz, n * c, n_h, dv)[:, :t_len]
    return o, s


def _ssd_chunked(x, dt, a, b_in, c_in, h0):
    bsz, t_len = x.shape[:2]
    c = min(SSD_CHUNK, t_len)
    n_pad = (-t_len) % c
    x, dt, b_in, c_in = [_pad_time(t.astype(jnp.float32), n_pad) for t in (x, dt, b_in, c_in)]
    n = (t_len + n_pad) // c
    gg, rr = SSD_GROUPS, SSD_HEADS_PER_GROUP

    def chunks(t):
        return jnp.swapaxes(t.reshape((bsz, n, c) + t.shape[2:]), 0, 1)

    xc = chunks(x).reshape(n, bsz, c, gg, rr, SSD_HEAD_DIM)
    dtc = chunks(dt).reshape(n, bsz, c, gg, rr)
    bc, cc = chunks(b_in), chunks(c_in)
    acum = jnp.cumsum(dtc * a.astype(jnp.float32).reshape(gg, rr), axis=2)
    causal = jnp.tril(jnp.ones((c, c), bool))[:, :, None, None]
    diff = acum[:, :, :, None] - acum[:, :, None, :]
    seg = jnp.where(causal, jnp.exp(jnp.where(causal, diff, 0.0)), 0.0)
    xdt = xc * dtc[..., None]
    cb = jnp.einsum('nbigs,nbjgs->nbijg', cc, bc)
    y_diag = jnp.einsum('nbijg,nbijgr,nbjgrp->nbigrp', cb, seg, xdt)
    decay_out = jnp.exp(acum)
    decay_in = jnp.exp(acum[:, :, -1:] - acum)
    decay_chunk = jnp.exp(acum[:, :, -1])

    def step(h, inp):
        c_i, b_i, xdt_i, dout_i, din_i, dch_i = inp
        y_off = jnp.einsum('bigs,bigr,bgrps->bigrp', c_i, dout_i, h)
        h = h * dch_i[..., None, None] + jnp.einsum('bjgs,bjgr,bjgrp->bgrps', b_i, din_i, xdt_i)
        return h, y_off

    h0 = h0.astype(jnp.float32).reshape(bsz, gg, rr, SSD_HEAD_DIM, SSD_STATE)
    h, y_off = lax.scan(step, h0, (cc, bc, xdt, decay_out, decay_in, decay_chunk))
    y = jnp.swapaxes(y_diag + y_off, 0, 1).reshape(bsz, n * c, SSD_HEADS, SSD_HEAD_DIM)[:, :t_len]
    return y, h.reshape(bsz, SSD_HEADS, SSD_HEAD_DIM, SSD_STATE)


def _pool_mix(u, buf, pos0, pool_w, pool_scale):
    t_len = u.shape[1]
    uf = u.astype(jnp.float32)
    cat = jnp.concatenate([buf.astype(jnp.float32), uf], axis=1)
    cs = jnp.pad(jnp.cumsum(cat, axis=1), ((0, 0), (1, 0), (0, 0)))
    pos = pos0 + jnp.arange(t_len)
    outs = []
    for gi, w in enumerate(POOL_WINDOWS):
        sl = slice(gi * POOL_GROUP_DIM, (gi + 1) * POOL_GROUP_DIM)
        hi = cs[:, POOL_BUF + 1:POOL_BUF + 1 + t_len, sl]
        lo = cs[:, POOL_BUF + 1 - w:POOL_BUF + 1 - w + t_len, sl]
        cnt = jnp.minimum(w, pos + 1).astype(jnp.float32)
        outs.append((hi - lo) / cnt[None, :, None])
    pooled = jnp.concatenate(outs, axis=-1) - uf
    pooled = pooled.reshape(u.shape[0], t_len, POOL_GROUPS, POOL_GROUP_DIM)
    mixed = jnp.einsum('btgc,gcd->btgd', pooled, pool_w.astype(jnp.float32))
    y = mixed.reshape(u.shape[0], t_len, D_POOL) * pool_scale.astype(jnp.float32)
    return y, cat[:, -POOL_BUF:].astype(buf.dtype)


def _mixer_layer(x, states, pos0, norm_w, w_in, gdn_conv_w, gdn_a_log, gdn_dt_bias, gdn_norm_w,
                 ssd_conv_w, ssd_conv_b, ssd_a_log, ssd_dt_bias, ssd_d, ssd_norm_w,
                 pool_w, pool_scale, w_out):
    gdn_s, gdn_buf, ssd_h, ssd_buf, pool_buf = states
    bsz, t_len, _ = x.shape
    h = _rmsnorm(x, norm_w)
    proj = jnp.einsum('btd,de->bte', h, w_in)
    offs = np.cumsum(IN_SIZES)[:-1].tolist()
    qkv, z_gdn, b_gdn, a_gdn, z_ssd, xbc, dt_ssd, u_pool, z_pool = jnp.split(proj, offs, axis=-1)

    qkv, gdn_buf_new = _causal_conv(qkv, gdn_buf, gdn_conv_w, None)
    qkv = jax.nn.silu(qkv)
    q, k, v = [qkv[..., i * D_GDN:(i + 1) * D_GDN].reshape(bsz, t_len, GDN_HEADS, GDN_HEAD_DIM)
               for i in range(3)]
    q = _l2norm(q) * (GDN_HEAD_DIM ** -0.5)
    k = _l2norm(k)
    beta = jax.nn.sigmoid(b_gdn.astype(jnp.float32))
    g = -jnp.exp(gdn_a_log.astype(jnp.float32)) * jax.nn.softplus(
        a_gdn.astype(jnp.float32) + gdn_dt_bias.astype(jnp.float32))
    o, gdn_s_new = _gated_delta_chunked(q, k, v, beta, g, gdn_s)
    z_g = z_gdn.reshape(bsz, t_len, GDN_HEADS, GDN_HEAD_DIM).astype(jnp.float32)
    y_gdn = (_rmsnorm(o, gdn_norm_w) * jax.nn.silu(z_g)).reshape(bsz, t_len, D_GDN)

    xbc, ssd_buf_new = _causal_conv(xbc, ssd_buf, ssd_conv_w, ssd_conv_b)
    xbc = jax.nn.silu(xbc)
    gs = SSD_GROUPS * SSD_STATE
    x_s = xbc[..., :D_SSD].reshape(bsz, t_len, SSD_HEADS, SSD_HEAD_DIM)
    b_s = xbc[..., D_SSD:D_SSD + gs].reshape(bsz, t_len, SSD_GROUPS, SSD_STATE)
    c_s = xbc[..., D_SSD + gs:].reshape(bsz, t_len, SSD_GROUPS, SSD_STATE)
    dt = jax.nn.softplus(dt_ssd.astype(jnp.float32) + ssd_dt_bias.astype(jnp.float32))
    a = -jnp.exp(ssd_a_log.astype(jnp.float32))
    y_s, ssd_h_new = _ssd_chunked(x_s, dt, a, b_s, c_s, ssd_h)
    y_s = y_s + ssd_d.astype(jnp.float32)[:, None] * x_s.astype(jnp.float32)
    y_s = y_s.reshape(bsz, t_len, D_SSD) * jax.nn.silu(z_ssd.astype(jnp.float32))
    y_ssd = _rmsnorm(y_s.reshape(bsz, t_len, SSD_GROUPS, D_SSD // SSD_GROUPS),
                     ssd_norm_w.reshape(SSD_GROUPS, D_SSD // SSD_GROUPS)).reshape(bsz, t_len, D_SSD)

    y_p, pool_buf_new = _pool_mix(u_pool, pool_buf, pos0, pool_w, pool_scale)
    y_pool = y_p * jax.nn.silu(z_pool.astype(jnp.float32))

    y_cat = jnp.concatenate([y_gdn, y_ssd, y_pool], axis=-1).astype(x.dtype)
    x = x + jnp.einsum('bte,ed->btd', y_cat, w_out)
    new_states = (gdn_s_new.astype(gdn_s.dtype), gdn_buf_new.astype(gdn_buf.dtype),
                  ssd_h_new.astype(ssd_h.dtype), ssd_buf_new.astype(ssd_buf.dtype), pool_buf_new)
    return x, new_states


def _dt_bias(key, shape):
    dt = jnp.exp(jax.random.uniform(key, shape, jnp.float32, math.log(1e-3), math.log(1e-1)))
    return dt + jnp.log(-jnp.expm1(-dt))


def setup_inputs(seed: int = 0) -> dict:
    key = jax.random.key(seed)
    ks = list(jax.random.split(key, 24))
    f32 = jnp.float32

    def nrm(i, shape, scale):
        return jax.random.normal(ks[i], shape, f32) * scale

    return {
        'x_prompt': nrm(0, (BATCH, SEQ, D_MODEL), 1.0),
        'x_sample': nrm(1, (DEC_BATCH, DEC_SEQ, D_MODEL), 1.0),
        'state_gdn': nrm(2, (DEPTH, DEC_BATCH, GDN_HEADS, GDN_HEAD_DIM, GDN_HEAD_DIM), 0.1),
        'state_gdn_conv': nrm(3, (DEPTH, DEC_BATCH, CONV_K - 1, 3 * D_GDN), 1.0),
        'state_ssd': nrm(4, (DEPTH, DEC_BATCH, SSD_HEADS, SSD_HEAD_DIM, SSD_STATE), 0.1),
        'state_ssd_conv': nrm(5, (DEPTH, DEC_BATCH, CONV_K - 1, SSD_CONV_CH), 1.0),
        'state_pool': nrm(6, (DEPTH, DEC_BATCH, POOL_BUF, D_POOL), 1.0),
        'norm_w': 1.0 + nrm(7, (DEPTH, D_MODEL), 0.02),
        'w_in': nrm(8, (DEPTH, D_MODEL, D_IN_PROJ), D_MODEL ** -0.5),
        'gdn_conv_w': nrm(9, (DEPTH, CONV_K, 3 * D_GDN), CONV_K ** -0.5),
        'gdn_a_log': jnp.log(jax.random.uniform(ks[10], (DEPTH, GDN_HEADS), f32, 1.0, 16.0)),
        'gdn_dt_bias': _dt_bias(ks[11], (DEPTH, GDN_HEADS)),
        'gdn_norm_w': 1.0 + nrm(12, (DEPTH, GDN_HEAD_DIM), 0.02),
        'ssd_conv_w': nrm(13, (DEPTH, CONV_K, SSD_CONV_CH), CONV_K ** -0.5),
        'ssd_conv_b': nrm(14, (DEPTH, SSD_CONV_CH), 0.01),
        'ssd_a_log': jnp.log(jax.random.uniform(ks[15], (DEPTH, SSD_HEADS), f32, 1.0, 16.0)),
        'ssd_dt_bias': _dt_bias(ks[16], (DEPTH, SSD_HEADS)),
        'ssd_d': 1.0 + nrm(17, (DEPTH, SSD_HEADS), 0.02),
        'ssd_norm_w': 1.0 + nrm(18, (DEPTH, D_SSD), 0.02),
        'pool_w': nrm(19, (DEPTH, POOL_GROUPS, POOL_GROUP_DIM, POOL_GROUP_DIM), POOL_GROUP_DIM ** -0.5),
        'pool_scale': 1.0 + nrm(20, (DEPTH, D_POOL), 0.02),
        'w_out': nrm(21, (DEPTH, D_MIX, D_MODEL), D_MIX ** -0.5),
        'final_norm_w': 1.0 + nrm(22, (D_MODEL,), 0.02),
    }


def reference(x_prompt, x_sample, state_gdn, state_gdn_conv, state_ssd, state_ssd_conv, state_pool,
              norm_w, w_in, gdn_conv_w, gdn_a_log, gdn_dt_bias, gdn_norm_w,
              ssd_conv_w, ssd_conv_b, ssd_a_log, ssd_dt_bias, ssd_d, ssd_norm_w,
              pool_w, pool_scale, w_out, final_norm_w):
    dtp = x_prompt.dtype
    zero_states = (
        jnp.zeros((BATCH, GDN_HEADS, GDN_HEAD_DIM, GDN_HEAD_DIM), dtp),
        jnp.zeros((BATCH, CONV_K - 1, 3 * D_GDN), dtp),
        jnp.zeros((BATCH, SSD_HEADS, SSD_HEAD_DIM, SSD_STATE), dtp),
        jnp.zeros((BATCH, CONV_K - 1, SSD_CONV_CH), dtp),
        jnp.zeros((BATCH, POOL_BUF, D_POOL), dtp),
    )
    xp, xs = x_prompt, x_sample
    new_p, new_s = [], []
    for l in range(DEPTH):
        lw = (norm_w[l], w_in[l], gdn_conv_w[l], gdn_a_log[l], gdn_dt_bias[l], gdn_norm_w[l],
              ssd_conv_w[l], ssd_conv_b[l], ssd_a_log[l], ssd_dt_bias[l], ssd_d[l], ssd_norm_w[l],
              pool_w[l], pool_scale[l], w_out[l])
        xp, sp = _mixer_layer(xp, zero_states, 0, *lw)
        past = (state_gdn[l], state_gdn_conv[l], state_ssd[l], state_ssd_conv[l], state_pool[l])
        xs, ss = _mixer_layer(xs, past, PAST_LEN, *lw)
        new_p.append(sp)
        new_s.append(ss)
    y_prompt = _rmsnorm(xp, final_norm_w)
    y_sample = _rmsnorm(xs, final_norm_w)
    new_gdn_p = jnp.stack([s[0] for s in new_p])
    new_gdn_conv_p = jnp.stack([s[1] for s in new_p])
    new_ssd_p = jnp.stack([s[2] for s in new_p])
    new_ssd_conv_p = jnp.stack([s[3] for s in new_p])
    new_pool_p = jnp.stack([s[4] for s in new_p])
    new_gdn_s = jnp.stack([s[0] for s in new_s])
    new_gdn_conv_s = jnp.stack([s[1] for s in new_s])
    new_ssd_s = jnp.stack([s[2] for s in new_s])
    new_ssd_conv_s = jnp.stack([s[3] for s in new_s])
    new_pool_s = jnp.stack([s[4] for s in new_s])
    return (y_prompt, y_sample, new_gdn_p, new_gdn_conv_p, new_ssd_p, new_ssd_conv_p, new_pool_p,
            new_gdn_s, new_gdn_conv_s, new_ssd_s, new_ssd_conv_s, new_pool_s)
```

```python
import contextlib
import numpy as np
import concourse.bass as bass
import concourse.mybir as mybir
from concourse.bass_utils import run_bass_kernel_spmd

F32 = mybir.dt.float32
F32R = mybir.dt.float32r
BF16 = mybir.dt.bfloat16
AF = mybir.ActivationFunctionType
ALU = mybir.AluOpType

D = 2048
DEPTH = 4
SEQ = 2048
NSEQ = 16
DTK = 4
SW = NSEQ * DTK
DIN = 6168
C_GDN = 2304
C_SSD = 1280
EPS = 1e-6
BIG = 30000.0
OFF_QKV, OFF_ZG, OFF_B, OFF_A, OFF_ZS, OFF_XBC, OFF_DT, OFF_UP, OFF_ZP = 0, 2304, 3072, 3078, 3084, 3852, 5132, 5144, 5656
BLK_COLS = ([OFF_QKV + i * 128 for i in range(18)] + [OFF_ZG + i * 128 for i in range(6)] + [OFF_ZS + i * 128 for i in range(6)]
            + [OFF_XBC + i * 128 for i in range(10)] + [OFF_UP + i * 128 for i in range(4)] + [OFF_ZP + i * 128 for i in range(4)])
BLK_OF = {c: i for i, c in enumerate(BLK_COLS)}
NBLK = len(BLK_COLS)
PT = 512
NPT = PT // 128
NPASS = SEQ // PT
SWP = 128
NSQP = SWP // DTK
TOKMAX = PT + SWP
NTM = NPT + 1
DBG_TI = 0
SELF_SYNC_SKIP = ()


class Buf:
    def __init__(self, name, t):
        self.name = name
        self.t = t
        self.w = None
        self.r = {}
        self.gw = []
        self.gr = []
        self.excl = False

    def __getitem__(self, k):
        return self.t[k]


class Group:
    def __init__(self, name, sem):
        self.name = name
        self.sem = sem
        self.total = 0
        self.waited = False


class Sched:
    ENGS = ("pe", "act", "dve", "pool", "sp")

    def __init__(self, nc, stack):
        self.nc = nc
        self.stack = stack
        self.eng = {"pe": nc.tensor, "act": nc.scalar, "dve": nc.vector,
                    "pool": nc.gpsimd, "sp": nc.sync}
        self.sem = {e: stack.enter_context(nc.semaphore("s_" + e)) for e in self.ENGS}
        self.cnt = {e: 0 for e in self.ENGS}
        self.seen = {e: {} for e in self.ENGS}
        self.groups = []
        self.same_engine_sync = True
        self.nwait = 0

    def sb(self, name, shape, dt, stack=None):
        self.nalloc = getattr(self, "nalloc", 0) + 1
        uname = "%s_%d" % (name, self.nalloc)
        return Buf(uname, (stack or self.stack).enter_context(self.nc.sbuf_tensor(uname, list(shape), dt)))

    def ps(self, name, shape, dt):
        b = Buf(name, self.stack.enter_context(self.nc.psum_tensor(name, list(shape), dt)))
        b.excl = True
        return b

    def dram(self, name, shape, dt, kind):
        return Buf(name, self.nc.dram_tensor(name, list(shape), dt, kind=kind).ap())

    def group(self, name):
        g = Group(name, self.stack.enter_context(self.nc.semaphore("g_" + name)))
        self.groups.append(g)
        return g

    def _need(self, e, key, sem, val):
        if val <= 0 or self.seen[e].get(key, 0) >= val:
            return
        self.eng[e].wait_ge(sem, val)
        self.nwait += 1
        self.seen[e][key] = val

    def _wait_eng(self, e, dep):
        if dep is None:
            return
        f, c = dep
        if f == e and (e == "pe" or not self.same_engine_sync or e in SELF_SYNC_SKIP):
            return
        self._need(e, f, self.sem[f], c)

    def _wait_grp(self, e, g):
        if g.total > 0:
            self._need(e, g.name, g.sem, g.total)
            g.waited = True

    def _deps(self, e, reads, writes):
        for b in reads:
            self._wait_eng(e, b.w)
            if b.excl:
                for f, c in b.r.items():
                    self._wait_eng(e, (f, c))
            for g in b.gw:
                self._wait_grp(e, g)
        for b in writes:
            self._wait_eng(e, b.w)
            for f, c in b.r.items():
                self._wait_eng(e, (f, c))
            for g in b.gw:
                self._wait_grp(e, g)
            for g in b.gr:
                self._wait_grp(e, g)

    def op(self, e, fn, R=(), W=()):
        self._deps(e, R, W)
        ins = fn(self.eng[e])
        self.cnt[e] += 1
        c = self.cnt[e]
        ins.then_inc(self.sem[e], 1)
        for b in R:
            b.r[e] = c
        for b in W:
            b.w = (e, c)
            b.r = {}
            b.gw = []
            b.gr = []
        return ins

    def dma(self, q, g, out_ap, in_ap, src, dst, **kw):
        self._deps(q, [src], [dst])
        if g.waited:
            self._wait_grp(q, g)
            g.waited = False
        ins = self.eng[q].dma_start(out=out_ap, in_=in_ap, **kw)
        ins.then_inc(g.sem, 16)
        g.total += 16
        if g not in src.gr:
            src.gr.append(g)
        dst.w = None
        dst.r = {}
        if g not in dst.gw:
            dst.gw.append(g)
        return ins

    def mark(self, name):
        if not hasattr(self, "marks"):
            self.marks = []
        self.marks.append((name, dict(self.cnt)))

    def barrier(self):
        for e in self.ENGS:
            for g in self.groups:
                self._wait_grp(e, g)
            for f in self.ENGS:
                if f != e and self.cnt[f] > 0:
                    self._need(e, f, self.sem[f], self.cnt[f])

    def finish(self, e="sp"):
        for g in self.groups:
            self._wait_grp(e, g)
        for f in self.ENGS:
            if f != e and self.cnt[f] > 0:
                self._need(e, f, self.sem[f], self.cnt[f])

    def mm(self, out, lhsT, rhs, R, W, start=True, stop=True):
        return self.op("pe", lambda e: e.matmul(out, lhsT=lhsT, rhs=rhs, start=start, stop=stop), R, W)

    def act(self, out, in_, func, R, W, bias=None, scale=None, accum_out=None):
        kw = {}
        if bias is not None:
            kw["bias"] = bias
        if scale is not None:
            kw["scale"] = scale
        if accum_out is not None:
            kw["accum_out"] = accum_out
        return self.op("act", lambda e: e.activation(out, in_, func, **kw), R, W)

    def tt(self, eng, out, in0, in1, op, R, W):
        return self.op(eng, lambda e: e.tensor_tensor(out=out, in0=in0, in1=in1, op=op), R, W)

    def stt(self, eng, out, in0, scalar, in1, op0, op1, R, W):
        return self.op(eng, lambda e: e.scalar_tensor_tensor(out=out, in0=in0, scalar=scalar, in1=in1, op0=op0, op1=op1), R, W)

    def ts(self, eng, out, in0, s1, op0, R, W):
        return self.op(eng, lambda e: e.tensor_scalar(out=out, in0=in0, scalar1=s1, scalar2=None, op0=op0), R, W)

    def cp(self, eng, out, in_, R, W):
        if eng == "act":
            return self.op("act", lambda e: e.activation(out, in_, AF.Copy), R, W)
        return self.op(eng, lambda e: e.tensor_copy(out, in_), R, W)

    def ms(self, eng, ap, val, W):
        return self.op(eng, lambda e: e.memset(ap, val), (), W)


def make_consts():
    c = {}
    i = np.arange(128)
    c["ident"] = np.eye(128, dtype=np.float32)
    c["ones"] = np.ones((128, 128), np.float32)
    c["U_p"] = (i[:, None] <= i[None, :]).astype(np.float32)
    c["NEG_p"] = np.where(i[None, :] < i[:, None], -BIG, 0.0).astype(np.float32)
    c["STR_p"] = (i[None, :] > i[:, None]).astype(np.float32)
    c["SEQ_p"] = np.ones((128, 2), np.float32)
    j = np.arange(128)
    sid = j // DTK
    valid = j < SW
    same = (sid[:, None] == sid[None, :]) & valid[:, None] & valid[None, :]
    c["U_s"] = (same & (j[:, None] <= j[None, :])).astype(np.float32)
    c["NEG_s"] = np.where(same & (j[None, :] >= j[:, None]), 0.0, -BIG).astype(np.float32)
    c["STR_s"] = (same & (j[None, :] > j[:, None])).astype(np.float32)
    sm = np.zeros((128, NSEQ), np.float32)
    sm[j[valid], sid[valid]] = 1.0
    c["SEQ_s"] = sm
    c["SAME_s"] = same.astype(np.float32)
    smask = np.zeros((128, NSEQ, SWP), np.float32)
    for s in range(NSEQ):
        smask[:, s, s * DTK:(s + 1) * DTK] = 1.0
    c["SMASK"] = smask
    wins = (2, 4, 8, 16)
    bc = np.zeros((4, 128, 128), np.float32)
    b0 = np.zeros((4, 128, 128), np.float32)
    bp = np.zeros((4, 128, 128), np.float32)
    bs = np.zeros((4, 128, 128), np.float32)
    bh = np.zeros((4, 2, 128, 128), np.float32)
    for g, w in enumerate(wins):
        for t in range(128):
            for k in range(w):
                tp = t - k
                if tp >= 0:
                    bc[g, tp, t] += 1.0 / w
                    b0[g, tp, t] += 1.0 / min(w, t + 1)
                else:
                    bp[g, 128 + tp, t] += 1.0 / w
            bc[g, t, t] -= 1.0
            b0[g, t, t] -= 1.0
        for s in range(NSEQ):
            for t in range(DTK):
                col = s * DTK + t
                for k in range(w):
                    tp = t - k
                    if tp >= 0:
                        bs[g, s * DTK + tp, col] += 1.0 / w
                    else:
                        r = 15 + tp
                        bh[g, s // 8, (s % 8) * 15 + r, col] += 1.0 / w
                bs[g, col, col] -= 1.0
    c["BC"] = bc.transpose(1, 0, 2).copy()
    c["B0"] = b0.transpose(1, 0, 2).copy()
    c["BP"] = bp.transpose(1, 0, 2).copy()
    c["BS"] = bs.transpose(1, 0, 2).copy()
    c["BH"] = bh.transpose(2, 0, 1, 3).reshape(128, 8, 128).copy()
    return c


CONST_SHAPES = {"ident": [128, 128], "ones": [128, 128], "U_p": [128, 128], "NEG_p": [128, 128], "STR_p": [128, 128],
                "SEQ_p": [128, 2], "U_s": [128, 128], "NEG_s": [128, 128], "STR_s": [128, 128], "SEQ_s": [128, NSEQ],
                "SAME_s": [128, 128], "SMASK": [128, NSEQ, SWP], "BC": [128, 4, 128], "B0": [128, 4, 128],
                "BP": [128, 4, 128], "BS": [128, 4, 128], "BH": [128, 8, 128]}
R_CONSTS = ("ident", "ones", "U_p", "NEG_p", "U_s", "NEG_s", "SAME_s")
F_CONSTS = ("STR_p", "STR_s", "SEQ_p", "SEQ_s")
P_CONSTS = ("BC", "B0", "BP", "BS", "BH")


def build(npass=NPASS, depth=DEPTH, parts=("gdn", "ssd", "pool"), debug=False):
    nc = bass.Bass("TRN2", target_bir_lowering=False)
    st = contextlib.ExitStack()
    with st:
        S = Sched(nc, st)
        _build(nc, S, npass, depth, parts, debug)
        S.finish()
        nc._marks = getattr(S, "marks", [])
        nc._cnt = dict(S.cnt)
        nc._nwait = S.nwait
    return nc


def _build(nc, S, npass, depth, parts, debug):
    din = {}

    def inp(name, shape):
        din[name] = S.dram(name, shape, F32, "ExternalInput")
        return din[name]

    xp_d = inp("x_prompt", [SEQ, D])
    xs_d = inp("x_sample", [SW, D])
    sg_d = inp("state_gdn", [DEPTH, NSEQ, 6, 128, 128])
    sgc_d = inp("state_gdn_conv", [DEPTH, NSEQ * 3, C_GDN])
    ss_d = inp("state_ssd", [DEPTH, NSEQ, 12, 64, 128])
    ssc_d = inp("state_ssd_conv", [DEPTH, NSEQ * 3, C_SSD])
    sp_d = inp("state_pool", [DEPTH, NSEQ, 15, 512])
    normw_d = inp("norm_w", [DEPTH, 128, 16])
    win_d = inp("w_in", [DEPTH, NBLK, 128, 16, 128])
    wsm_d = inp("w_small", [DEPTH, 128, 16, 24])
    gcw_d = inp("gdn_conv_w", [DEPTH, 128, 18, 4])
    galog_d = inp("gdn_a_log", [DEPTH, 6])
    gdtb_d = inp("gdn_dt_bias", [DEPTH, 6])
    gnw_d = inp("gdn_norm_w", [DEPTH, 128, 1])
    scw_d = inp("ssd_conv_w", [DEPTH, 128, 10, 4])
    scb_d = inp("ssd_conv_b", [DEPTH, 128, 10])
    salog_d = inp("ssd_a_log", [DEPTH, 12])
    sdtb_d = inp("ssd_dt_bias", [DEPTH, 12])
    sd_d = inp("ssd_d", [DEPTH, 12])
    snw_d = inp("ssd_norm_w", [DEPTH, 128, 6])
    pw_d = inp("pool_w", [DEPTH, 128, 4, 128])
    psc_d = inp("pool_scale", [DEPTH, 128, 4])
    wout_d = inp("w_out", [DEPTH, 16, 128, 16, 128])
    fnw_d = inp("final_norm_w", [128, 16])
    cd = {k: inp("c_" + k, shp) for k, shp in CONST_SHAPES.items()}

    def outp(name, shape):
        return S.dram(name, shape, F32, "ExternalOutput")

    yp_d = outp("y_prompt", [SEQ, D])
    ys_d = outp("y_sample", [SW, D])
    ngp_d = outp("new_gdn_p", [DEPTH, 6, 128, 128])
    ngcp_d = outp("new_gdn_conv_p", [DEPTH, 3, C_GDN])
    nsp_d = outp("new_ssd_p", [DEPTH, 6, 128, 128])
    nscp_d = outp("new_ssd_conv_p", [DEPTH, 3, C_SSD])
    npp_d = outp("new_pool_p", [DEPTH, 15, 512])
    ngs_d = outp("new_gdn_s", [DEPTH, NSEQ, 6, 128, 128])
    ngcs_d = outp("new_gdn_conv_s", [DEPTH, NSEQ * 3, C_GDN])
    nss_d = outp("new_ssd_s", [DEPTH, NSEQ, 6, 128, 128])
    nscs_d = outp("new_ssd_conv_s", [DEPTH, NSEQ * 3, C_SSD])
    nps_d = outp("new_pool_s", [DEPTH, NSEQ, 15, 512])
    dbg_d = outp("dbg", [128, 8192]) if debug else None
    car_g = S.dram("car_g", [DEPTH, 6, 128, 128], F32, "Internal")
    car_gc = S.dram("car_gc", [DEPTH, 18, 128, 3], F32, "Internal")
    car_s = S.dram("car_s", [DEPTH, 6, 128, 128], F32, "Internal")
    car_sc = S.dram("car_sc", [DEPTH, 10, 128, 3], F32, "Internal")
    car_p = S.dram("car_p", [DEPTH, 4, 128, 128], F32, "Internal")

    xT = S.sb("xT", [128, 16, TOKMAX], F32)
    hT = S.sb("hT", [128, 16, TOKMAX], BF16)
    ycat = S.sb("ycat", [128, 16, TOKMAX], BF16)
    cr = {k: S.sb("r_" + k, CONST_SHAPES[k], F32R) for k in R_CONSTS}
    cf = {k: S.sb("f_" + k, CONST_SHAPES[k], F32) for k in F_CONSTS}
    ident_b = S.sb("ident_b", [128, 128], BF16)
    cf["SMASK"] = S.sb("f_SMASK", CONST_SHAPES["SMASK"], BF16)
    epsc = S.sb("epsc", [128, 1], F32)
    onec = S.sb("onec", [128, 1], F32)
    NWB = 3
    wb = [S.sb("wb%d" % i, [128, 16, 128], BF16) for i in range(NWB)]
    gwb = [S.group("wb%d" % i) for i in range(NWB)]
    banks = [S.ps("bank%d" % i, [128, 512], F32) for i in range(8)]
    gmisc = S.group("misc")
    gwsm = S.group("wsm")
    gout = S.group("out")
    gcar = S.group("carry")
    gstate = S.group("state")
    scr8 = S.sb("scr8", [128, 2048], F32)
    rstd = S.sb("rstd", [128, 512], F32)
    normw = S.sb("normw", [128, 16], F32)
    gcw = S.sb("gcw", [128, 18, 4], F32)
    scw = S.sb("scw", [128, 10, 4], F32)
    scb = S.sb("scb", [128, 10], F32)
    gnw = S.sb("gnw", [128, 1], F32)
    snw = S.sb("snw", [128, 6], F32)
    psc = S.sb("psc", [128, 4], F32)
    rowc = S.sb("rowc", [128, 48], F32)
    negA = S.sb("negA", [128, 18], F32)
    fnw = S.sb("fnw", [128, 16], F32)
    wsm = S.sb("wsm", [128, 16, 24], BF16)
    diag = None
    small = S.sb("small", [128, NTM, 24], F32)
    GD = S.sb("GD", [128, NTM, 18], F32R)
    negGD = S.sb("negGD", [128, NTM, 18], F32R)
    cum = S.sb("cum", [128, NTM, 18], F32)
    EX1 = S.sb("EX1", [128, NTM, 18], F32)
    EX2 = S.sb("EX2", [128, NTM, 18], F32)
    EX3 = S.sb("EX3", [128, NTM, 18], F32)
    tot = S.sb("tot", [128, NTM, 18], F32)
    beta = S.sb("beta", [128, NTM, 6], F32)
    nbeta = S.sb("nbeta", [128, NTM, 6], F32)
    dtv = S.sb("dtv", [128, NTM, 12], F32)
    glbc = S.sb("glbc", [128, NSEQ, 18], F32)
    gdm = S.sb("gdm", [128, NSEQ, 18], F32)
    tmp18 = S.sb("tmp18", [128, 18], F32)
    tmp18b = S.sb("tmp18b", [128, 18], F32)
    ctok_p = S.sb("ctok_p", [3, 128], F32)
    ctok_s = S.sb("ctok_s", [NSEQ * 3, 128], F32)
    hst_tok = S.sb("hst_tok", [NSEQ * 3, 128], F32)
    pre = S.sb("pre", [128, 4 + PT], F32R)
    pre_s = S.sb("pre_s", [128, NSQP, 8], F32R)
    hal32 = S.sb("hal32", [128, 3], F32)
    cacc = S.sb("cacc", [128, 512], F32)
    tail32 = S.sb("tail32", [128, 3], F32)
    tails32 = S.sb("tails32", [128, NSEQ, 3], F32)

    sqb = S.sb("sqb", [128, 2048], F32R)
    sq = sqb.t[:, :].rearrange("p (c n) -> p c n", c=16)
    xtok = scr8.t

    def I32v(k):
        return cr[k].t.bitcast(F32)

    for k in R_CONSTS:
        S.dma("sp", gmisc, scr8[:, 0:128], cd[k][:], cd[k], scr8)
        S.cp("dve", cr[k][:], scr8[:, 0:128], [scr8], [cr[k]])
    for k in F_CONSTS:
        S.dma("sp", gmisc, cf[k][:], cd[k][:], cd[k], cf[k])
    S.cp("dve", ident_b[:], I32v("ident")[:], [cr["ident"]], [ident_b])
    S.dma("sp", gmisc, scr8[:, :], cd["SMASK"][:].rearrange("p a b -> p (a b)"), cd["SMASK"], scr8)
    S.cp("dve", cf["SMASK"][:].rearrange("p a b -> p (a b)"), scr8[:, :], [scr8], [cf["SMASK"]])
    S.ms("pool", epsc[:], EPS, [epsc])
    S.ms("pool", onec[:], 1.0, [onec])
    zero32 = S.sb("zero32", [128, 256], F32)
    S.ms("pool", zero32[:, :], 0.0, [zero32])
    S.cp("pool", pre_s[:, :, :], zero32[:, :].rearrange("p (s t) -> p s t", s=NSQP), [zero32], [pre_s])
    S.dma("sp", gmisc, fnw[:], fnw_d.t[:, :], fnw_d, fnw)

    bank_rr = [0]

    def dense_bank():
        bank_rr[0] ^= 1
        return banks[bank_rr[0]]

    wide_rr = [0]

    def wide_bank():
        wide_rr[0] = (wide_rr[0] + 1) % 8
        return banks[wide_rr[0]]

    wstate = {"n": 0}

    def load_block(src_ap, srcbuf):
        i = wstate["n"] % NWB
        wstate["n"] += 1
        S.dma("pool", gwb[i], wb[i][:], src_ap, srcbuf, wb[i])
        return wb[i]

    def win_ap(l, col0):
        return win_d.t[l, BLK_OF[col0]]

    def wout_ap(l, col0):
        return wout_d.t[l, col0 // 128]

    def rsq_from(dst, src, scale, R, W):
        S.act(dst, src, AF.Ln, R, W, bias=epsc[:, 0:1], scale=scale)
        S.act(dst, dst, AF.Exp, W, W, scale=-0.5)

    def dbg_dump(ap_sb, buf, col0, ncol, rows=128):
        if dbg_d is not None:
            S.dma("sp", gout, dbg_d.t[:rows, col0:col0 + ncol], ap_sb, buf, dbg_d)

    def load_x(src_d, row0, W, col0):
        S.dma("sp", gmisc, xtok[:W, :], src_d.t[row0:row0 + W, :], src_d, scr8)
        for c in range(16):
            b = dense_bank()
            S.mm(b[:, :W], xtok[:W, c * 128:(c + 1) * 128], I32v("ident")[:W, :W], [scr8, cr["ident"]], [b])
            S.cp("act" if c % 2 else "dve", xT[:, c, col0:col0 + W], b[:, :W], [b], [xT])

    def phase_a(tiles):
        for (c0, W, _) in tiles:
            S.act(sq[:, :, :W], xT[:, :, c0:c0 + W], AF.Square, [xT], [sqb])
            b = dense_bank()
            for c in range(16):
                S.mm(b[:, :W], cr["ones"][:], sq[:, c, :W], [cr["ones"], sqb], [b], start=(c == 0), stop=(c == 15))
            rsq_from(rstd[:, :W], b[:, :W], 1.0 / D, [b], [rstd])
            S.tt("dve", xtok[:, :16 * W].rearrange("p (c n) -> p c n", c=16), xT[:, :, c0:c0 + W],
                 rstd[:, :W].unsqueeze(1).to_broadcast([128, 16, W]), ALU.mult, [xT, rstd], [scr8])
            S.tt("dve", hT[:, :, c0:c0 + W], xtok[:, :16 * W].rearrange("p (c n) -> p c n", c=16),
                 normw[:].unsqueeze(2).to_broadcast([128, 16, W]), ALU.mult, [scr8, normw], [hT])

    def final_out(tiles, ps_i):
        for (c0, W, mode) in tiles:
            S.act(sq[:, :, :W], xT[:, :, c0:c0 + W], AF.Square, [xT], [sqb])
            b = dense_bank()
            for c in range(16):
                S.mm(b[:, :W], cr["ones"][:], sq[:, c, :W], [cr["ones"], sqb], [b], start=(c == 0), stop=(c == 15))
            rsq_from(rstd[:, :W], b[:, :W], 1.0 / D, [b], [rstd])
            yn = hT.t.bitcast(F32)
            ynv = yn[:, :, 0:W]
            S.tt("dve", ynv, xT[:, :, c0:c0 + W], rstd[:, :W].unsqueeze(1).to_broadcast([128, 16, W]), ALU.mult, [xT, rstd], [hT])
            S.tt("dve", ynv, ynv, fnw[:].unsqueeze(2).to_broadcast([128, 16, W]), ALU.mult, [hT, fnw], [hT])
            for c in range(16):
                bb = dense_bank()
                S.mm(bb[:W, :128], yn[:, c, 0:W], I32v("ident")[:, :], [hT, cr["ident"]], [bb])
                S.cp("act" if c % 2 else "dve", xtok[:W, c * 128:(c + 1) * 128], bb[:W, :128], [bb], [scr8])
            if mode == "p":
                S.dma("sp", gout, yp_d.t[ps_i * PT + c0:ps_i * PT + c0 + W, :], xtok[:W, :], scr8, yp_d)
            else:
                S.dma("sp", gout, ys_d.t[:, :], xtok[:SW, :], scr8, ys_d)

    def drain(gen):
        for _ in gen:
            pass

    def interleave(ga, gb):
        a_done = ga is None
        b_done = gb is None
        while not (a_done and b_done):
            if not a_done:
                try:
                    next(ga)
                except StopIteration:
                    a_done = True
            if not b_done:
                try:
                    next(gb)
                except StopIteration:
                    b_done = True

    def proj_block(wbuf, cgs, evac):
        for (c0, n) in cgs:
            b = dense_bank()
            for c in range(16):
                S.mm(b[:, :n], wbuf[:, c, :], hT[:, c, c0:c0 + n], [wbuf, hT], [b], start=(c == 0), stop=(c == 15))
            evac(c0, n, b)
            yield

    def conv_block(l, wbuf, cgs, ps_i, cw, cwi, bias_ap, dest_fn, car_d, car_i, hist_src, out_s, out_p, last_pass):
        if ps_i == 0:
            S.cp("dve", pre[:, 0:3], zero32[:, 0:3], [zero32], [pre])
        else:
            S.dma("sp", gcar, hal32[:], car_d.t[l, car_i], car_d, hal32)
            S.cp("dve", pre[:, 0:3], hal32[:], [hal32], [pre])
        has_s = any(c0 >= PT for (c0, _) in cgs)
        if has_s:
            b = dense_bank()
            S.dma("sp", gstate, hst_tok[:, :], hist_src[0], hist_src[1], hst_tok)
            S.mm(b[:, 0:NSEQ * 3], hst_tok[:, :], I32v("ident")[:NSEQ * 3, :NSEQ * 3], [hst_tok, cr["ident"]], [b])
            S.cp("act", pre_s[:, 0:NSEQ, 0:3], b[:, 0:NSEQ * 3].rearrange("p (s r) -> p s r", s=NSEQ), [b], [pre_s])

        def evac(c0, n, b):
            if c0 >= PT:
                S.cp("act", pre_s[:, :, 3:7], b[:, :SWP].rearrange("p (s t) -> p s t", s=NSQP), [b], [pre_s])
                S.cp("dve", tails32[:, :, :], b[:, :SW].rearrange("p (s t) -> p s t", s=NSEQ)[:, :, 1:4], [b], [tails32])
            else:
                S.cp("act", pre[:, 3 + c0:3 + c0 + n], b[:, :n], [b], [pre])
                if c0 + n == PT:
                    S.cp("dve", tail32[:, :], b[:, n - 3:n], [b], [tail32])
        yield from proj_block(wbuf, cgs, evac)
        if ps_i < last_pass:
            S.dma("sp", gcar, car_d.t[l, car_i], tail32[:], tail32, car_d)
        else:
            b = dense_bank()
            S.mm(b[:3, 128:256], tail32[:, :], I32v("ident")[:, :], [tail32, cr["ident"]], [b])
            S.cp("dve", ctok_p[:, :], b[:3, 128:256], [b], [ctok_p])
            S.dma("sp", gout, out_p[0], ctok_p[:, :], ctok_p, out_p[1])
        if has_s:
            b = dense_bank()
            S.mm(b[:NSEQ * 3, 256:384], tails32[:, :, :].rearrange("p s r -> p (s r)"), I32v("ident")[:, :], [tails32, cr["ident"]], [b])
            S.cp("dve", ctok_s[:, :], b[:NSEQ * 3, 256:384], [b], [ctok_s])
            S.dma("sp", gout, out_s[0], ctok_s[:, :], ctok_s, out_s[1])
        pre32 = pre.t.bitcast(F32)
        pres32 = pre_s.t.bitcast(F32)
        for (c0, n) in cgs:
            if c0 >= PT:
                av = cacc[:, :SWP].rearrange("p (s t) -> p s t", s=NSQP)
                S.ts("dve", av, pres32[:, :, 0:4], cw[:, cwi, 0:1], ALU.mult, [pre_s, cw], [cacc])
                for j in range(1, 4):
                    S.stt("dve", av, pres32[:, :, j:j + 4], cw[:, cwi, j:j + 1], av, ALU.mult, ALU.add, [pre_s, cw, cacc], [cacc])
            else:
                S.ts("dve", cacc[:, :n], pre32[:, c0:c0 + n], cw[:, cwi, 0:1], ALU.mult, [pre, cw], [cacc])
                for j in range(1, 4):
                    S.stt("dve", cacc[:, :n], pre32[:, c0 + j:c0 + j + n], cw[:, cwi, j:j + 1], cacc[:, :n], ALU.mult, ALU.add,
                          [pre, cw, cacc], [cacc])
            dap, dbuf = dest_fn(c0, n)
            if bias_ap is None:
                S.act(dap, cacc[:, :n], AF.Silu, [cacc], [dbuf])
            else:
                S.act(dap, cacc[:, :n], AF.Silu, [cacc, scb], [dbuf], bias=bias_ap)
            yield

    def gdn_phase(l, ps_i, tiles, cgs, last_pass):
        NT = len(tiles)
        with contextlib.ExitStack() as ph:
            def A(name, shape, dt):
                return S.sb(name, shape, dt, stack=ph)
            HB = [dict(qkn=A("qkn%d" % i, [128, 2, TOKMAX], F32R), vr=A("vr%d" % i, [128, TOKMAX], F32R),
                       zs=A("zs%d" % i, [128, TOKMAX], F32)) for i in range(2)]
            Xs = [A("Xs%d" % i, [128, 128], F32R) for i in range(NTM)]
            Es = [A("Es%d" % i, [128, 128], F32) for i in range(NTM)]
            QT = [A("QT%d" % i, [128, 384], F32R) for i in range(NTM)]
            Tfin = A("Tfin", [128, NTM, 128], F32R)
            QKT = A("QKT", [128, NTM, 128], F32R)
            kg = A("kg", [128, NTM, 128], F32R)
            kdec = A("kdec", [128, NTM, 128], F32R)
            vtok = A("vtok", [128, NTM, 128], F32R)
            ub = A("ub", [128, NTM, 128], F32)
            wT = A("wT", [128, NTM, 128], F32R)
            Sst = A("Sst", [128, 128], F32)
            Sr = A("Sr", [128, 128], F32R)
            vnew = A("vnew", [128, 128], F32R)
            qse = Es[0]
            otoks = [A("otok%d" % i, [128, 128], F32) for i in range(2)]
            onr = A("onr", [128, 128], F32R)
            ssq1 = A("ssq1", [128, 1], F32)
            Ssm = A("Ssm", [128, NSEQ, 128], F32)
            Ssr = A("Ssr", [128, 8, 128], F32R)
            wqm = A("wqm", [128, 8, 128], F32R)
            kdm = wqm
            GD32 = GD.t.bitcast(F32)
            sq2 = sqb.t[:, :].rearrange("p (a n) -> p a n", a=4)
            bs = banks[2]

            def proj_gen(h):
                qkn, vr, zs = HB[h % 2]["qkn"], HB[h % 2]["vr"], HB[h % 2]["zs"]
                qkn32 = qkn.t.bitcast(F32)
                for which in range(3):
                    cidx = which * 6 + h
                    col = OFF_QKV + cidx * 128
                    wbuf = load_block(win_ap(l, col), win_d)
                    if which < 2:
                        dfn = (lambda c0, n, which=which: (qkn[:, which, c0:c0 + n], qkn))
                    else:
                        dfn = (lambda c0, n: (vr[:, c0:c0 + n], vr))
                    csl = slice(cidx * 128, (cidx + 1) * 128)
                    yield from conv_block(l, wbuf, cgs, ps_i, gcw, cidx, None, dfn, car_gc, cidx, (sgc_d.t[l, :, csl], sgc_d),
                                          (ngcs_d.t[l, :, csl], ngcs_d), (ngcp_d.t[l, :, csl], ngcp_d), last_pass)
                wbuf = load_block(win_ap(l, OFF_ZG + h * 128), win_d)
                yield from proj_block(wbuf, cgs, lambda c0, n, b: S.act(zs[:, c0:c0 + n], b[:, :n], AF.Silu, [b], [zs]))
                for (c0, n) in cgs:
                    S.act(sq2[:, 0:2, :n], qkn32[:, :, c0:c0 + n], AF.Square, [qkn], [sqb])
                    b1 = dense_bank()
                    S.mm(b1[:, :n], cr["ones"][:], sq2[:, 0, :n], [cr["ones"], sqb], [b1])
                    b2 = dense_bank()
                    S.mm(b2[:, :n], cr["ones"][:], sq2[:, 1, :n], [cr["ones"], sqb], [b2])
                    rsq_from(rstd[:, :n], b1[:, :n], 1.0, [b1], [rstd])
                    S.stt("dve", qkn[:, 0, c0:c0 + n], qkn32[:, 0, c0:c0 + n], 128.0 ** -0.5, rstd[:, :n], ALU.mult, ALU.mult,
                          [qkn, rstd], [qkn])
                    rsq_from(rstd[:, :n], b2[:, :n], 1.0, [b2], [rstd])
                    S.tt("dve", qkn[:, 1, c0:c0 + n], qkn32[:, 1, c0:c0 + n], rstd[:, :n], ALU.mult, [qkn, rstd], [qkn])
                    yield

            def chain_gen(h):
                qkn, vr, zs = HB[h % 2]["qkn"], HB[h % 2]["vr"], HB[h % 2]["zs"]
                def cm(ti):
                    c0, W, mode = tiles[ti]
                    if mode == "p":
                        return c0, W, cr["U_p"], cr["NEG_p"], cf["STR_p"], 6
                    return c0, W, cr["U_s"], cr["NEG_s"], cf["STR_s"], 1

                yield
                for ti in range(NT):
                    c0, W, U, NEG, STR, n_it = cm(ti)
                    S.ts("dve", Xs[ti][:W, :W], U.t.bitcast(F32)[:W, :W], GD32[:W, ti, h:h + 1], ALU.mult, [U, GD], [Xs[ti]])
                yield
                for ti in range(NT):
                    c0, W, U, NEG, STR, n_it = cm(ti)
                    bk = banks[3 + ti]
                    S.mm(bk[:W, 0:W], cr["ones"][:W, :W], Xs[ti][:W, :W], [cr["ones"], Xs[ti]], [bk], start=True, stop=False)
                    S.mm(bk[:W, 0:W], U[:W, :W], negGD[:W, ti, h:h + 1].to_broadcast([W, W]), [U, negGD], [bk], start=False, stop=False)
                    S.mm(bk[:W, 0:W], cr["ident"][:W, :W], NEG[:W, :W], [cr["ident"], NEG], [bk], start=False, stop=True)
                yield
                for ti in range(NT):
                    c0, W, U, NEG, STR, n_it = cm(ti)
                    bk = banks[3 + ti]
                    S.act(Es[ti][:W, :W], bk[:W, 0:W], AF.Exp, [bk], [Es[ti]])
                    S.tt("dve", Xs[ti][:W, :W], Es[ti][:W, :W], STR[:W, :W], ALU.mult, [Es[ti], STR], [Xs[ti]])
                yield
                for ti in range(NT):
                    c0, W, U, NEG, STR, n_it = cm(ti)
                    bk = banks[3 + ti]
                    kT = qkn[:, 1, c0:c0 + W]
                    qT = qkn[:, 0, c0:c0 + W]
                    S.mm(bk[:W, 128:384].rearrange("p (a i) -> p a i", a=2), kT, qkn[:, :, c0:c0 + W], [qkn], [bk])
                yield
                for ti in range(NT):
                    c0, W, U, NEG, STR, n_it = cm(ti)
                    bk = banks[3 + ti]
                    S.stt("dve", QT[ti][:W, 0:W], bk[:W, 256:256 + W], nbeta[:W, ti, h:h + 1], Xs[ti].t.bitcast(F32)[:W, :W], ALU.mult, ALU.mult,
                          [bk, nbeta, Xs[ti]], [QT[ti]])
                    S.tt("dve", QKT[:W, ti, :W], bk[:W, 128:128 + W], Es[ti][:W, :W], ALU.mult, [bk, Es[ti]], [QKT])
                yield
                for ti in range(NT):
                    c0, W, U, NEG, STR, n_it = cm(ti)
                    bk = banks[3 + ti]
                    S.mm(bk[:W, 384:384 + W], QT[ti][:W, 0:W], cr["ident"][:W, :W], [QT[ti], cr["ident"]], [bk])
                    S.mm(bk[:W, 0:128], qkn[:, 1, c0:c0 + W], cr["ident"][:, :], [qkn, cr["ident"]], [bk])
                    S.mm(bk[:W, 128:256], vr[:, c0:c0 + W], cr["ident"][:, :], [vr, cr["ident"]], [bk])
                yield
                for ti in range(NT):
                    c0, W, U, NEG, STR, n_it = cm(ti)
                    bk = banks[3 + ti]
                    S.cp("act", QT[ti][:W, 256:256 + W], bk[:W, 384:384 + W], [bk], [QT[ti]])
                    S.cp("dve", QT[ti][:W, 128:128 + W], I32v("ident")[:W, :W], [cr["ident"]], [QT[ti]])
                    S.act(kg[:W, ti, :], bk[:W, 0:128], AF.Copy, [bk, EX1], [kg], scale=EX1[:W, ti, h:h + 1])
                    S.act(kdec[:W, ti, :], bk[:W, 0:128], AF.Copy, [bk, EX2], [kdec], scale=EX2[:W, ti, h:h + 1])
                    S.cp("dve", vtok[:W, ti, :], bk[:W, 128:256], [bk], [vtok])
                for k in range(1, 7):
                    for ti in range(NT):
                        c0, W, U, NEG, STR, n_it = cm(ti)
                        if k > n_it:
                            continue
                        bk = banks[3 + ti]
                        S.mm(bk[:W, 0:256].rearrange("p (a i) -> p a i", a=2), QT[ti][:W, 256:256 + W],
                             QT[ti][:W, 0:256].rearrange("p (a i) -> p a i", a=2), [QT[ti]], [bk])
                        S.mm(bk[:W, 256:256 + W], QT[ti][:W, 0:W], QT[ti][:W, 256:256 + W], [QT[ti]], [bk])
                    yield
                    for ti in range(NT):
                        c0, W, U, NEG, STR, n_it = cm(ti)
                        if k > n_it:
                            continue
                        bk = banks[3 + ti]
                        S.cp("act", QT[ti][:W, :].rearrange("p (a b) -> p a b", a=3)[:, 0::2, :],
                             bk[:W, 0:384].rearrange("p (a b) -> p a b", a=3)[:, 0::2, :], [bk], [QT[ti]])
                        S.tt("dve", QT[ti][:W, 128:128 + W], QT[ti].t.bitcast(F32)[:W, 128:128 + W], bk[:W, 128:128 + W], ALU.add,
                             [QT[ti], bk], [QT[ti]])
                    yield
                yield
                for ti in range(NT):
                    c0, W, U, NEG, STR, n_it = cm(ti)
                    bk = banks[3 + ti]
                    S.mm(bk[:W, 128:128 + W], QT[ti][:W, 256:256 + W], QT[ti][:W, 128:128 + W], [QT[ti]], [bk])
                yield
                for ti in range(NT):
                    c0, W, U, NEG, STR, n_it = cm(ti)
                    bk = banks[3 + ti]
                    S.tt("dve", Tfin[:W, ti, :W], QT[ti].t.bitcast(F32)[:W, 128:128 + W], bk[:W, 128:128 + W], ALU.add, [QT[ti], bk], [Tfin])
                yield
                for ti in range(NT):
                    c0, W, U, NEG, STR, n_it = cm(ti)
                    bk = banks[3 + ti]
                    S.mm(bk[:W, 0:128], Tfin[:W, ti, :W], vtok[:W, ti, :], [Tfin, vtok], [bk])
                    S.mm(bk[:, 128:128 + W], kg[:W, ti, :], Tfin[:W, ti, :W], [kg, Tfin], [bk])
                yield
                for ti in range(NT):
                    c0, W, U, NEG, STR, n_it = cm(ti)
                    bk = banks[3 + ti]
                    S.act(ub[:W, ti, :], bk[:W, 0:128], AF.Copy, [bk, beta], [ub], scale=beta[:W, ti, h:h + 1])
                    S.cp("dve", wT[:, ti, :W], bk[:, 128:128 + W], [bk], [wT])

                if ps_i == 0:
                    S.ms("pool", Sst[:, :], 0.0, [Sst])
                else:
                    S.dma("sp", gcar, Sst[:, :], car_g.t[l, h], car_g, Sst)
                S.cp("act", Sr[:, :], Sst[:, :], [Sst], [Sr])

                def out_norm(ti):
                    c0, W, mode = tiles[ti]
                    otok = otoks[ti % 2]
                    S.act(onr[:W, :], otok[:W, :], AF.Square, [otok], [onr, ssq1], accum_out=ssq1[:W, 0:1])
                    rsq_from(ssq1[:W, :], ssq1[:W, :], 1.0 / 128, [ssq1], [ssq1])
                    S.ts("dve", onr[:W, :], otok[:W, :], ssq1[:W, 0:1], ALU.mult, [otok, ssq1], [onr])
                    bd = dense_bank()
                    S.mm(bd[:, :W], onr[:W, :], cr["ident"][:W, :W], [onr, cr["ident"]], [bd])
                    S.stt("dve", ycat[:, h, c0:c0 + W], bd[:, :W], gnw[:, 0:1], zs[:, c0:c0 + W], ALU.mult, ALU.mult,
                          [bd, gnw, zs], [ycat])

                def mid(ti, W, qs_ap=None, qs_buf=None):
                    if qs_ap is None:
                        qs_ap, qs_buf = bs[:W, 128:256], bs
                    S.stt("dve", vnew[:W, :], bs[:W, 0:128], nbeta[:W, ti, h:h + 1], ub[:W, ti, :], ALU.mult, ALU.add,
                          [bs, nbeta, ub], [vnew])
                    S.act(qse[:W, :], qs_ap, AF.Copy, [qs_buf, EX1], [qse], scale=EX1[:W, ti, h:h + 1])
                    S.mm(bs[:W, 256:384], QKT[:W, ti, :W], vnew[:W, :], [QKT, vnew], [bs])
                    S.tt("dve", otoks[ti % 2][:W, :], bs[:W, 256:384], qse[:W, :], ALU.add, [bs, qse], [otoks[ti % 2]])

                yield
                for ti in range(NT):
                    c0, W, mode = tiles[ti]
                    if mode != "p":
                        continue
                    S.mm(bs[:W, 0:128], wT[:, ti, :W], Sr[:, :], [wT, Sr], [bs])
                    S.mm(bs[:W, 128:256], qkn[:, 0, c0:c0 + W], Sr[:, :], [qkn, Sr], [bs])
                    if ti > 0:
                        out_norm(ti - 1)
                    yield
                    mid(ti, W)
                    yield
                    S.mm(bs[:, 384:512], kdec[:W, ti, :], vnew[:W, :], [kdec, vnew], [bs])
                    S.stt("dve", Sst[:, :], Sst[:, :], EX3[:, ti, h:h + 1], bs[:, 384:512], ALU.mult, ALU.add, [Sst, EX3, bs], [Sst])
                    S.cp("act", Sr[:, :], Sst[:, :], [Sst], [Sr])
                    yield
                out_norm(NPT - 1)
                if ps_i < last_pass:
                    S.dma("sp", gcar, car_g.t[l, h], Sst[:, :], Sst, car_g)
                else:
                    S.dma("sp", gout, ngp_d.t[l, h], Sst[:, :], Sst, ngp_d)

                yield
                for ti in range(NT):
                    c0, W, mode = tiles[ti]
                    if mode != "s":
                        continue
                    S.dma("sp", gstate, Ssm[:, :, :], sg_d.t[l, :, h].rearrange("s k v -> k s v"), sg_d, Ssm)
                    bq = banks[7]
                    for half in range(2):
                        sl8 = slice(half * 8, half * 8 + 8)
                        S.cp("act", Ssr[:, :, :], Ssm[:, sl8, :], [Ssm], [Ssr])
                        S.tt("dve", wqm[:, :, :], wT.t.bitcast(F32)[:, ti, :].unsqueeze(1).to_broadcast([128, 8, 128]),
                             cf["SMASK"][:, sl8, :], ALU.mult, [wT, cf["SMASK"]], [wqm])
                        for s8 in range(8):
                            S.mm(bs[:, 0:128], wqm[:, s8, :], Ssr[:, s8, :], [wqm, Ssr], [bs],
                                 start=(half == 0 and s8 == 0), stop=(half == 1 and s8 == 7))
                        S.tt("dve", wqm[:, :, :], qkn.t.bitcast(F32)[:, 0, c0:c0 + 128].unsqueeze(1).to_broadcast([128, 8, 128]),
                             cf["SMASK"][:, sl8, :], ALU.mult, [qkn, cf["SMASK"]], [wqm])
                        for s8 in range(8):
                            S.mm(bq[:, 0:128], wqm[:, s8, :], Ssr[:, s8, :], [wqm, Ssr], [bq],
                                 start=(half == 0 and s8 == 0), stop=(half == 1 and s8 == 7))
                    mid(ti, W, bq[:, 0:128], bq)
                    yield
                    for half in range(2):
                        sl8 = slice(half * 8, half * 8 + 8)
                        S.tt("dve", kdm[:, :, :], kdec.t.bitcast(F32)[:, ti, :].unsqueeze(1).to_broadcast([128, 8, 128]),
                             cf["SEQ_s"][:, sl8].unsqueeze(2).to_broadcast([128, 8, 128]), ALU.mult, [kdec, cf["SEQ_s"]], [kdm])
                        for q4 in range(2):
                            bk = banks[3 + q4]
                            for s4 in range(4):
                                S.mm(bk[:, s4 * 128:(s4 + 1) * 128], kdm[:, q4 * 4 + s4, :], vnew[:, :], [kdm, vnew], [bk])
                            sl4 = slice(half * 8 + q4 * 4, half * 8 + q4 * 4 + 4)
                            S.tt("dve", Ssm[:, sl4, :], Ssm[:, sl4, :], glbc[:, sl4, h:h + 1].to_broadcast([128, 4, 128]), ALU.mult,
                                 [Ssm, glbc], [Ssm])
                            S.tt("dve", Ssm[:, sl4, :], Ssm[:, sl4, :], bk[:, :].rearrange("p (s v) -> p s v", s=4), ALU.add, [Ssm, bk], [Ssm])
                    S.dma("sp", gout, ngs_d.t[l, :, h].rearrange("s k v -> k s v"), Ssm[:, :, :], Ssm, ngs_d)
                    out_norm(ti)
                yield
            drain(proj_gen(0))
            for h in range(6):
                interleave(chain_gen(h), proj_gen(h + 1) if h + 1 < 6 else None)
            S.barrier()

    def ssd_phase(l, ps_i, tiles, cgs, last_pass):
        NT = len(tiles)
        with contextlib.ExitStack() as ph:
            def A(name, shape, dt):
                return S.sb(name, shape, dt, stack=ph)
            xs = A("xs", [128, 3, TOKMAX], BF16)
            Bs = A("Bs", [128, TOKMAX], BF16)
            Cs = A("Cs", [128, TOKMAX], BF16)
            zss = A("zss", [128, 3, TOKMAX], F32)
            xdtP = [A("xdt%d" % i, [128, 6, 64], F32R) for i in range(2)]
            xDP = [A("xD%d" % i, [128, 6, 64], F32) for i in range(2)]
            xdtwP = [A("xdtw%d" % i, [128, 6, 64], F32R) for i in range(2)]
            BtokP = [A("Btok%d" % i, [128, 128], F32R) for i in range(2)]
            X6 = A("X6", [128, 6, 128], F32R)
            E6 = A("E6", [128, 6, 128], F32)
            MT6 = A("MT6", [128, 6, 128], F32R)
            Hst = A("Hst", [128, 3, 128], F32)
            HTb = A("HTb", [128, 384], BF16)
            yoff = A("yoff", [128, 6, 64], F32)
            ytok = A("ytok", [128, 384], F32)
            ytr = A("ytr", [128, 384], F32R)
            yg = A("yg", [128, 3, 128], F32)
            sqg = A("sqg", [128, 3, 128], F32R)
            dtaexp = A("dtaexp", [128, 6, 64], F32)
            dch = A("dch", [128, 3, 2], F32)
            dchs = A("dchs", [128, 3, NSEQ], F32)
            Hsm = A("Hsm", [128, NSEQ, 128], F32)
            HTs = A("HTs", [128, NSEQ, 128], BF16)
            Cm = A("Cm", [128, NSEQ, 128], BF16)
            xdtwm = A("xdtwm", [128, NSEQ, 128], F32R)
            GD32 = GD.t.bitcast(F32)
            b2, b3, b4, b5, b6, b7 = banks[2], banks[3], banks[4], banks[5], banks[6], banks[7]

            for g in range(2):
                blocks = [("x", 3 * g + blk, blk) for blk in range(3)] + [("B", 6 + g, 0), ("C", 8 + g, 0)]
                for (kind, cidx, blk) in blocks:
                    wbuf = load_block(win_ap(l, OFF_XBC + cidx * 128), win_d)
                    if kind == "x":
                        dfn = (lambda c0, n, blk=blk: (xs[:, blk, c0:c0 + n], xs))
                    elif kind == "B":
                        dfn = (lambda c0, n: (Bs[:, c0:c0 + n], Bs))
                    else:
                        dfn = (lambda c0, n: (Cs[:, c0:c0 + n], Cs))
                    csl = slice(cidx * 128, (cidx + 1) * 128)
                    drain(conv_block(l, wbuf, cgs, ps_i, scw, cidx, scb[:, cidx:cidx + 1], dfn, car_sc, cidx, (ssc_d.t[l, :, csl], ssc_d),
                                     (nscs_d.t[l, :, csl], nscs_d), (nscp_d.t[l, :, csl], nscp_d), last_pass))
                for blk in range(3):
                    wbuf = load_block(win_ap(l, OFF_ZS + (3 * g + blk) * 128), win_d)
                    drain(proj_block(wbuf, cgs, lambda c0, n, b, blk=blk: S.act(zss[:, blk, c0:c0 + n], b[:, :n], AF.Silu, [b], [zss])))
                for blk in range(3):
                    if ps_i == 0:
                        S.ms("pool", Hst[:, blk, :], 0.0, [Hst])
                    else:
                        S.dma("sp", gcar, Hst[:, blk, :], car_s.t[l, 3 * g + blk], car_s, Hst)
                gc0 = 6 + 6 * g

                def pre(ti):
                        c0, W, mode = tiles[ti]
                        U, NEG = (cr["U_p"], cr["NEG_p"]) if mode == "p" else (cr["U_s"], cr["NEG_s"])
                        xdt_, xD_, xdtw_, Btok_ = xdtP[ti % 2], xDP[ti % 2], xdtwP[ti % 2], BtokP[ti % 2]
                        yb = banks[ti % 2]
                        for blk in range(3):
                            S.mm(b3[:W, blk * 128:(blk + 1) * 128], xs[:, blk, c0:c0 + W], ident_b[:, :], [xs, ident_b], [b3])
                        S.mm(b3[:W, 384:512], Bs[:, c0:c0 + W], ident_b[:, :], [Bs, ident_b], [b3])
                        xv = b3[:W, 0:384].rearrange("p (h q) -> p h q", h=6)
                        S.tt("dve", xdt_[:W, :, :], xv, dtv[:W, ti, 6 * g:6 * g + 6].unsqueeze(2).to_broadcast([W, 6, 64]), ALU.mult, [b3, dtv], [xdt_])
                        S.tt("dve", xD_[:W, :, :], xv, rowc[:W, 36 + 6 * g:42 + 6 * g].unsqueeze(2).to_broadcast([W, 6, 64]), ALU.mult, [b3, rowc], [xD_])
                        S.cp("act", Btok_[:W, :], b3[:W, 384:512], [b3], [Btok_])
                        S.tt("dve", xdtw_[:W, :, :], xdt_.t.bitcast(F32)[:W, :, :], EX2[:W, ti, gc0:gc0 + 6].unsqueeze(2).to_broadcast([W, 6, 64]),
                             ALU.mult, [xdt_, EX2], [xdtw_])
                        S.mm(b4[:W, 0:W], Bs[:, c0:c0 + W], Cs[:, c0:c0 + W], [Bs, Cs], [b4])
                        S.tt("dve", X6[:W, :, :W], U.t.bitcast(F32)[:W, :W].unsqueeze(1).to_broadcast([W, 6, W]),
                             GD32[:W, ti, gc0:gc0 + 6].unsqueeze(2).to_broadcast([W, 6, W]), ALU.mult, [U, GD], [X6])
                        for (bk, h0, h1) in ((b5, 0, 4), (b6, 4, 6)):
                            nh = h1 - h0
                            ov = bk[:W, 0:nh * 128].rearrange("p (h i) -> p h i", h=nh)[:, :, :W]
                            S.mm(ov, cr["ones"][:W, :W], X6[:W, h0:h1, :W], [cr["ones"], X6], [bk], start=True, stop=False)
                            S.mm(ov, U[:W, :W], negGD[:W, ti, gc0 + h0:gc0 + h1].unsqueeze(2).to_broadcast([W, nh, W]), [U, negGD], [bk],
                                 start=False, stop=False)
                            S.mm(ov, cr["ident"][:W, :W], NEG[:W, :W].unsqueeze(1).to_broadcast([W, nh, W]), [cr["ident"], NEG], [bk],
                                 start=False, stop=True)
                            S.act(E6[:W, h0:h1, :W], ov, AF.Exp, [bk], [E6])
                        S.tt("dve", MT6[:W, :, :W], E6[:W, :, :W], b4[:W, 0:W].unsqueeze(1).to_broadcast([W, 6, W]), ALU.mult, [E6, b4], [MT6])
                        for hh in range(6):
                            S.mm(yb[:W, hh * 64:(hh + 1) * 64], MT6[:W, hh, :W], xdt_[:W, hh, :], [MT6, xdt_], [yb])

                def post(ti):
                        c0, W, mode = tiles[ti]
                        U, NEG = (cr["U_p"], cr["NEG_p"]) if mode == "p" else (cr["U_s"], cr["NEG_s"])
                        xdt_, xD_, xdtw_, Btok_ = xdtP[ti % 2], xDP[ti % 2], xdtwP[ti % 2], BtokP[ti % 2]
                        yb = banks[ti % 2]
                        if mode == "p":
                            for blk in range(3):
                                S.mm(b2[:, blk * 128:(blk + 1) * 128], Hst[:, blk, :], I32v("ident")[:, :], [Hst, cr["ident"]], [b2])
                            S.cp("act", HTb[:, :], b2[:, 0:384], [b2], [HTb])
                            S.mm(b2[:W, 0:384], Cs[:, c0:c0 + W], HTb[:, :], [Cs, HTb], [b2])
                        else:
                            S.tt("dve", Cm[:, :, :], Cs[:, c0:c0 + 128].unsqueeze(1).to_broadcast([128, NSEQ, 128]), cf["SMASK"][:, :, :], ALU.mult,
                                 [Cs, cf["SMASK"]], [Cm])
                            S.op("act", lambda e: e.activation(dtaexp[:W, :, :], GD32[:W, ti, gc0:gc0 + 6].unsqueeze(2).to_broadcast([W, 6, 64]), AF.Copy), [GD], [dtaexp])
                            for blk in range(3):
                                hb = 6 * g + 2 * blk
                                S.dma("sp", gstate, Hsm[:, :, :], ss_d.t[l, :, hb:hb + 2].rearrange("s h p n -> (h p) s n"), ss_d, Hsm)
                                for q4 in range(4):
                                    bk = banks[3 + q4]
                                    for s4 in range(4):
                                        s = q4 * 4 + s4
                                        S.mm(bk[:, s4 * 128:(s4 + 1) * 128], Hsm[:, s, :], I32v("ident")[:, :], [Hsm, cr["ident"]], [bk])
                                    S.cp("act", HTs[:, q4 * 4:(q4 + 1) * 4, :], bk[:, :].rearrange("p (s v) -> p s v", s=4), [bk], [HTs])
                                for s in range(NSEQ):
                                    S.mm(b2[:, blk * 128:(blk + 1) * 128], Cm[:, s, :], HTs[:, s, :], [Cm, HTs], [b2],
                                         start=(s == 0), stop=(s == NSEQ - 1))
                                S.tt("dve", xdtwm[:, :, :], xdtw_.t.bitcast(F32)[:, 2 * blk:2 * blk + 2, :].rearrange("p h q -> p (h q)").unsqueeze(1).to_broadcast([128, NSEQ, 128]),
                                     cf["SEQ_s"][:, :].unsqueeze(2).to_broadcast([128, NSEQ, 128]), ALU.mult, [xdtw_, cf["SEQ_s"]], [xdtwm])
                                S.mm(b7[:, 400 + blk * 16:416 + blk * 16], dtaexp[:, 2 * blk:2 * blk + 2, :].rearrange("p h q -> p (h q)"), cf["SEQ_s"][:, :],
                                     [dtaexp, cf["SEQ_s"]], [b7])
                                S.act(dchs[:, blk, :], b7[:, 400 + blk * 16:416 + blk * 16], AF.Exp, [b7], [dchs])
                                for q4 in range(4):
                                    bk = banks[3 + q4]
                                    for s4 in range(4):
                                        s = q4 * 4 + s4
                                        S.mm(bk[:, s4 * 128:(s4 + 1) * 128], xdtwm[:, s, :], Btok_[:, :], [xdtwm, Btok_], [bk])
                                    sl4 = slice(q4 * 4, (q4 + 1) * 4)
                                    S.tt("dve", Hsm[:, sl4, :], Hsm[:, sl4, :], dchs[:, blk, sl4].unsqueeze(2).to_broadcast([128, 4, 128]), ALU.mult,
                                         [Hsm, dchs], [Hsm])
                                    S.tt("dve", Hsm[:, sl4, :], Hsm[:, sl4, :], bk[:, :].rearrange("p (s v) -> p s v", s=4), ALU.add, [Hsm, bk], [Hsm])
                                S.dma("sp", gout, nss_d.t[l, :, 3 * g + blk].rearrange("s r n -> r s n"), Hsm[:, :, :], Hsm, nss_d)
                        S.tt("dve", yoff[:W, :, :], b2[:W, 0:384].rearrange("p (h q) -> p h q", h=6),
                             EX1[:W, ti, gc0:gc0 + 6].unsqueeze(2).to_broadcast([W, 6, 64]), ALU.mult, [b2, EX1], [yoff])
                        S.tt("dve", ytok[:W, :], yb[:W, 0:384], yoff[:W, :, :].rearrange("p h q -> p (h q)"), ALU.add, [yb, yoff], [ytok])
                        S.tt("dve", ytr[:W, :], ytok[:W, :], xD_[:W, :, :].rearrange("p h q -> p (h q)"), ALU.add, [ytok, xD_], [ytr])
                        if dbg_d is not None and l == 0 and ps_i == 0 and g == 0 and ti == DBG_TI:
                            dbg_dump(xdt_.t.bitcast(F32)[:, :, :].rearrange("p h q -> p (h q)"), xdt_, 256, 384)
                            dbg_dump(E6[:, 0, :], E6, 640, 128)
                            dbg_dump(MT6.t.bitcast(F32)[:, 0, :], MT6, 768, 128)
                            dbg_dump(ytok[:, :], ytok, 896, 384)
                            dbg_dump(Btok_.t.bitcast(F32)[:, :], Btok_, 1280, 128)
                            dbg_dump(yoff[:, :, :].rearrange("p h q -> p (h q)"), yoff, 1408, 384)
                            dbg_dump(ytr.t.bitcast(F32)[:, :], ytr, 1792, 384)
                        if mode == "p":
                            S.op("act", lambda e: e.activation(dtaexp[:W, :, :], GD32[:W, ti, gc0:gc0 + 6].unsqueeze(2).to_broadcast([W, 6, 64]), AF.Copy), [GD], [dtaexp])
                            for blk in range(3):
                                S.mm(b4[:, 256 + blk * 2:258 + blk * 2], dtaexp[:W, 2 * blk:2 * blk + 2, :].rearrange("p h q -> p (h q)"), cf["SEQ_p"][:W, 0:2],
                                     [dtaexp, cf["SEQ_p"]], [b4])
                            S.act(dch[:, :, :], b4[:, 256:262].rearrange("p (b t) -> p b t", b=3), AF.Exp, [b4], [dch])
                            for blk in range(3):
                                S.mm(b5[:, blk * 128:(blk + 1) * 128], xdtw_[:W, 2 * blk:2 * blk + 2, :].rearrange("p h q -> p (h q)"), Btok_[:W, :], [xdtw_, Btok_], [b5])
                            for blk in range(3):
                                S.stt("dve", Hst[:, blk, :], Hst[:, blk, :], dch[:, blk, 0:1], b5[:, blk * 128:(blk + 1) * 128], ALU.mult, ALU.add,
                                      [Hst, dch, b5], [Hst])
                        for blk in range(3):
                            S.mm(b6[:, blk * 128:blk * 128 + W], ytr[:W, blk * 128:(blk + 1) * 128], cr["ident"][:W, :W], [ytr, cr["ident"]], [b6])
                        S.tt("dve", yg[:, :, :W], b6[:, 0:384].rearrange("p (b i) -> p b i", b=3)[:, :, :W], zss[:, :, c0:c0 + W], ALU.mult, [b6, zss], [yg])
                        S.act(sqg[:, :, :W], yg[:, :, :W], AF.Square, [yg], [sqg])
                        for blk in range(3):
                            S.mm(b7[:, 384:384 + W], cr["ones"][:, :], sqg[:, blk, :W], [cr["ones"], sqg], [b7], start=(blk == 0), stop=(blk == 2))
                        rsq_from(rstd[:, :W], b7[:, 384:384 + W], 1.0 / 384, [b7], [rstd])
                        for blk in range(3):
                            S.stt("dve", ycat[:, 6 + 3 * g + blk, c0:c0 + W], yg[:, blk, :W], snw[:, 3 * g + blk:3 * g + blk + 1], rstd[:, :W],
                                  ALU.mult, ALU.mult, [yg, snw, rstd], [ycat])

                pre(0)
                for ti in range(NT):
                    if ti + 1 < NT:
                        pre(ti + 1)
                    post(ti)
                for blk in range(3):
                    if ps_i < last_pass:
                        S.dma("sp", gcar, car_s.t[l, 3 * g + blk], Hst[:, blk, :], Hst, car_s)
                    else:
                        S.dma("sp", gout, nsp_d.t[l, 3 * g + blk], Hst[:, blk, :], Hst, nsp_d)
            S.barrier()

    def pool_phase(l, ps_i, tiles, cgs, last_pass):
        NT = len(tiles)
        with contextlib.ExitStack() as ph:
            def A(name, shape, dt):
                return S.sb(name, shape, dt, stack=ph)
            pc = {k: A("p_" + k, CONST_SHAPES[k], F32R) for k in P_CONSTS}
            pw = A("pw", [128, 4, 128], F32)
            pwr = A("pwr", [128, 4, 128], F32R)
            utok = A("utok", [128, NTM, 128], F32R)
            utok32 = A("utok32", [128, NTM, 128], F32)
            uprev = A("uprev", [128, 128], F32)
            uprev_r = A("uprev_r", [128, 128], F32R)
            zp = A("zp", [128, TOKMAX], F32)
            uT = A("uT", [128, TOKMAX], F32R)
            pooledT = A("pooledT", [128, 128], F32R)
            hist = A("hist", [128, 2, 128], F32)
            hist_r = A("hist_r", [128, 2, 128], F32R)
            S.ms("pool", hist[:, :, :], 0.0, [hist])
            for k in P_CONSTS:
                n = CONST_SHAPES[k][1] * 128
                S.dma("sp", gmisc, scr8[:, 0:n], cd[k][:].rearrange("p a b -> p (a b)"), cd[k], scr8)
                S.cp("dve", pc[k][:].rearrange("p a b -> p (a b)"), scr8[:, 0:n], [scr8], [pc[k]])
            S.dma("sp", gmisc, pw[:], pw_d.t[l], pw_d, pw)
            S.cp("dve", pwr[:], pw[:], [pw], [pwr])
            for g in range(4):
                wbuf = load_block(win_ap(l, OFF_UP + g * 128), win_d)
                drain(proj_block(wbuf, cgs, lambda c0, n, b: S.cp("act", uT[:, c0:c0 + n], b[:, :n], [b], [uT])))
                for ti in range(NT):
                    c0, W, mode = tiles[ti]
                    b = dense_bank()
                    S.mm(b[:W, 0:128], uT[:, c0:c0 + W], cr["ident"][:, :], [uT, cr["ident"]], [b])
                    S.cp("act", utok32[:W, ti, :], b[:W, 0:128], [b], [utok32])
                    S.cp("dve", utok[:W, ti, :], b[:W, 0:128], [b], [utok])
                wbuf = load_block(win_ap(l, OFF_ZP + g * 128), win_d)
                drain(proj_block(wbuf, cgs, lambda c0, n, b: S.act(zp[:, c0:c0 + n], b[:, :n], AF.Silu, [b], [zp])))
                if ps_i > 0:
                    S.dma("sp", gcar, uprev[:, :], car_p.t[l, g], car_p, uprev)
                    S.cp("dve", uprev_r[:, :], uprev[:, :], [uprev], [uprev_r])
                for ti in range(NT):
                    c0, W, mode = tiles[ti]
                    bk = banks[3 + ti]
                    if mode == "p":
                        first = (ps_i == 0 and ti == 0)
                        band = pc["B0"] if first else pc["BC"]
                        S.mm(bk[:, 0:W], utok[:W, ti, :], band[:W, g, :W], [utok, band], [bk], start=True, stop=first)
                        if not first:
                            if ti == 0:
                                S.mm(bk[:, 0:W], uprev_r[:, :], pc["BP"][:, g, :W], [uprev_r, pc["BP"]], [bk], start=False, stop=True)
                            else:
                                S.mm(bk[:, 0:W], utok[:, ti - 1, :], pc["BP"][:, g, :W], [utok, pc["BP"]], [bk], start=False, stop=True)
                    else:
                        for half in range(2):
                            S.dma("sp", gstate, hist[:120, half, :],
                                  sp_d.t[l, half * 8:(half + 1) * 8, :, g * 128:(g + 1) * 128].rearrange("s r c -> (s r) c"), sp_d, hist)
                        S.cp("dve", hist_r[:, :, :], hist[:, :, :], [hist], [hist_r])
                        S.mm(bk[:, 0:W], utok[:W, ti, :], pc["BS"][:W, g, :W], [utok, pc["BS"]], [bk], start=True, stop=False)
                        S.mm(bk[:, 0:W], hist_r[:, 0, :], pc["BH"][:, g * 2, :W], [hist_r, pc["BH"]], [bk], start=False, stop=False)
                        S.mm(bk[:, 0:W], hist_r[:, 1, :], pc["BH"][:, g * 2 + 1, :W], [hist_r, pc["BH"]], [bk], start=False, stop=True)
                        S.dma("sp", gout, nps_d.t[l, :, 0:11, g * 128:(g + 1) * 128], sp_d.t[l, :, 4:15, g * 128:(g + 1) * 128], sp_d, nps_d)
                        for s in range(NSEQ):
                            S.dma("sp", gout, nps_d.t[l, s, 11:15, g * 128:(g + 1) * 128],
                                  utok32[s * DTK:(s + 1) * DTK, ti, :], utok32, nps_d)
                    S.cp("act", pooledT[:, :W], bk[:, 0:W], [bk], [pooledT])
                    S.mm(bk[:, 128:128 + W], pwr[:, g, :], pooledT[:, :W], [pwr, pooledT], [bk])
                    S.stt("dve", ycat[:, 12 + g, c0:c0 + W], bk[:, 128:128 + W], psc[:, g:g + 1], zp[:, c0:c0 + W], ALU.mult, ALU.mult,
                          [bk, psc, zp], [ycat])
                lt = NPT - 1
                if ps_i < last_pass:
                    S.dma("sp", gcar, car_p.t[l, g], utok32[:, lt, :], utok32, car_p)
                else:
                    S.dma("sp", gout, npp_d.t[l, :, g * 128:(g + 1) * 128], utok32[113:128, lt, :], utok32, npp_d)
            S.barrier()

    for ps_i in range(npass):
        last_pass = npass - 1
        tiles = [(t * 128, 128, "p") for t in range(NPT)]
        if ps_i == 0:
            tiles.append((PT, SWP, "s"))
        TOK = PT + (SWP if ps_i == 0 else 0)
        NT = len(tiles)
        cgs = [(0, PT)] + ([(PT, SWP)] if ps_i == 0 else [])
        for t in range(NPT):
            load_x(xp_d, ps_i * PT + t * 128, 128, t * 128)
        if ps_i == 0:
            load_x(xs_d, 0, SW, PT)
            S.ms("pool", xT[:, :, PT + SW:PT + SWP], 0.0, [xT])

        for l in range(depth):
            S.dma("sp", gmisc, normw[:], normw_d.t[l], normw_d, normw)
            S.dma("sp", gmisc, gcw[:], gcw_d.t[l], gcw_d, gcw)
            S.dma("sp", gmisc, scw[:], scw_d.t[l], scw_d, scw)
            S.dma("sp", gmisc, scb[:], scb_d.t[l], scb_d, scb)
            S.dma("sp", gmisc, gnw[:], gnw_d.t[l], gnw_d, gnw)
            S.dma("sp", gmisc, snw[:], snw_d.t[l], snw_d, snw)
            S.dma("sp", gmisc, psc[:], psc_d.t[l], psc_d, psc)
            for (o, n, srcd) in ((0, 6, galog_d), (6, 6, gdtb_d), (12, 12, salog_d), (24, 12, sdtb_d), (36, 12, sd_d)):
                S.dma("sp", gmisc, rowc[:, o:o + n], srcd.t[l:l + 1, :].to_broadcast([128, n]), srcd, rowc)
            S.act(negA[:, 0:6], rowc[:, 0:6], AF.Exp, [rowc], [negA])
            S.act(negA[:, 6:18], rowc[:, 12:24], AF.Exp, [rowc], [negA])
            S.ts("dve", negA[:], negA[:], -1.0, ALU.mult, [negA], [negA])
            S.dma("pool", gwsm, wsm[:, :, :], wsm_d.t[l], wsm_d, wsm)

            S.mark("L%d.%d params" % (ps_i, l))
            phase_a(tiles)
            S.mark("L%d.%d phaseA" % (ps_i, l))

            for ti, (c0, W, mode) in enumerate(tiles):
                b = banks[2]
                for c in range(16):
                    S.mm(b[:W, 0:24], hT[:, c, c0:c0 + W], wsm[:, c, :], [hT, wsm], [b], start=(c == 0), stop=(c == 15))
                S.cp("dve", small[:W, ti, :], b[:W, 0:24], [b], [small])
                sm = small[:W, ti, :]
                S.act(beta[:W, ti, :], sm[:, 0:6], AF.Exp, [small], [beta], scale=-1.0)
                S.ts("dve", beta[:W, ti, :], beta[:W, ti, :], 1.0, ALU.add, [beta], [beta])
                S.op("dve", lambda e: e.reciprocal(beta[:W, ti, :], beta[:W, ti, :]), [beta], [beta])
                S.ts("dve", nbeta[:W, ti, :], beta[:W, ti, :], -1.0, ALU.mult, [beta], [nbeta])
                S.tt("dve", tmp18[:W, 0:6], sm[:, 6:12], rowc[:W, 6:12], ALU.add, [small, rowc], [tmp18])
                S.tt("dve", tmp18[:W, 6:18], sm[:, 12:24], rowc[:W, 24:36], ALU.add, [small, rowc], [tmp18])
                S.act(tmp18b[:W, :], tmp18[:W, :], AF.Abs, [tmp18], [tmp18b])
                S.act(tmp18b[:W, :], tmp18b[:W, :], AF.Exp, [tmp18b], [tmp18b], scale=-1.0)
                S.act(tmp18b[:W, :], tmp18b[:W, :], AF.Ln, [tmp18b], [tmp18b], bias=onec[:W, 0:1])
                S.stt("dve", tmp18[:W, :], tmp18[:W, :], 0.0, tmp18b[:W, :], ALU.max, ALU.add, [tmp18, tmp18b], [tmp18])
                S.cp("dve", dtv[:W, ti, :], tmp18[:W, 6:18], [tmp18], [dtv])
                S.tt("dve", GD[:W, ti, :], tmp18[:W, :], negA[:W, :], ALU.mult, [tmp18, negA], [GD])
                S.ts("dve", negGD[:W, ti, :], GD.t.bitcast(F32)[:W, ti, :], -1.0, ALU.mult, [GD], [negGD])
                U = cr["U_p"] if mode == "p" else cr["U_s"]
                SM = cr["ones"] if mode == "p" else cr["SAME_s"]
                S.mm(b[:W, 32:50], U[:W, :W], GD[:W, ti, :], [U, GD], [b])
                S.mm(b[:W, 64:82], SM[:W, :W], GD[:W, ti, :], [SM, GD], [b])
                S.cp("dve", cum[:W, ti, :], b[:W, 32:50], [b], [cum])
                S.cp("dve", tot[:W, ti, :], b[:W, 64:82], [b], [tot])
                S.act(EX1[:W, ti, :], cum[:W, ti, :], AF.Exp, [cum], [EX1])
                S.act(EX3[:W, ti, :], tot[:W, ti, :], AF.Exp, [tot], [EX3])
                S.tt("dve", tmp18[:W, :], tot[:W, ti, :], cum[:W, ti, :], ALU.subtract, [tot, cum], [tmp18])
                S.act(EX2[:W, ti, :], tmp18[:W, :], AF.Exp, [tmp18], [EX2])
                if mode == "s":
                    S.tt("dve", gdm[:W, :, :], GD.t.bitcast(F32)[:W, ti, :].unsqueeze(1).to_broadcast([W, NSEQ, 18]),
                         cf["SEQ_s"][:W, :].unsqueeze(2).to_broadcast([W, NSEQ, 18]), ALU.mult, [GD, cf["SEQ_s"]], [gdm])
                    S.mm(b[:, 128:128 + NSEQ * 18], I32v("ones")[:W, :], gdm[:W, :, :].rearrange("p s h -> p (s h)"), [cr["ones"], gdm], [b])
                    S.act(glbc[:, :, :], b[:, 128:128 + NSEQ * 18].rearrange("p (s h) -> p s h", s=NSEQ), AF.Exp, [b], [glbc])

            if dbg_d is not None and l == 0 and ps_i == 0:
                for (o, n, bf) in ((0, 24, small), (24, 18, GD), (42, 18, cum), (60, 18, tot), (78, 18, EX1), (96, 18, EX2), (114, 12, dtv), (126, 6, beta)):
                    dbg_dump(bf.t.bitcast(F32)[:, DBG_TI, :] if bf is GD else bf[:, DBG_TI, :], bf, o, n)
            S.mark("L%d.%d small" % (ps_i, l))
            if "gdn" in parts:
                gdn_phase(l, ps_i, tiles, cgs, last_pass)
            else:
                S.ms("pool", ycat[:, 0:6, :], 0.0, [ycat])
            S.mark("L%d.%d gdn" % (ps_i, l))
            if "ssd" in parts:
                ssd_phase(l, ps_i, tiles, cgs, last_pass)
            else:
                S.ms("pool", ycat[:, 6:12, :], 0.0, [ycat])
            S.mark("L%d.%d ssd" % (ps_i, l))
            if "pool" in parts:
                pool_phase(l, ps_i, tiles, cgs, last_pass)
            else:
                S.ms("pool", ycat[:, 12:16, :], 0.0, [ycat])

            S.mark("L%d.%d pool" % (ps_i, l))
            for ob in range(16):
                wbuf = load_block(wout_ap(l, ob * 128), wout_d)
                for (c0, n) in cgs:
                    b = wide_bank()
                    for c in range(16):
                        S.mm(b[:, :n], wbuf[:, c, :], ycat[:, c, c0:c0 + n], [wbuf, ycat], [b], start=(c == 0), stop=(c == 15))
                    S.tt("dve", xT[:, ob, c0:c0 + n], xT[:, ob, c0:c0 + n], b[:, :n], ALU.add, [xT, b], [xT])

            S.mark("L%d.%d outproj" % (ps_i, l))
        final_out(tiles, ps_i)
        S.mark("L%d final" % ps_i)


_NC_CACHE = {}


def make_in_maps(inputs):
    consts = make_consts()
    maps = []
    f = lambda a: np.ascontiguousarray(np.asarray(a, dtype=np.float32))
    shared = {k: f(inputs[k]) for k in ("gdn_a_log", "gdn_dt_bias", "ssd_a_log", "ssd_dt_bias", "ssd_d")}
    w_in = np.asarray(inputs["w_in"], np.float32)
    wt = np.empty((DEPTH, NBLK, 128, 16, 128), np.float32)
    for b, c0 in enumerate(BLK_COLS):
        wt[:, b] = w_in[:, :, c0:c0 + 128].reshape(DEPTH, 16, 128, 128).transpose(0, 2, 1, 3)
    shared["w_in"] = wt
    wsmall = np.concatenate([w_in[:, :, OFF_B:OFF_B + 12], w_in[:, :, OFF_DT:OFF_DT + 12]], axis=2)
    shared["w_small"] = f(wsmall.reshape(DEPTH, 16, 128, 24).transpose(0, 2, 1, 3))
    w_out = np.asarray(inputs["w_out"], np.float32)
    shared["w_out"] = f(w_out.reshape(DEPTH, 16, 128, 16, 128).transpose(0, 3, 2, 1, 4))
    def pc(a, nb):
        a = np.asarray(a, np.float32)
        return f(a.reshape(a.shape[:-1] + (nb, 128)).swapaxes(-1, -2))
    shared["norm_w"] = pc(inputs["norm_w"], 16)
    shared["final_norm_w"] = pc(inputs["final_norm_w"], 16)
    shared["gdn_conv_w"] = f(np.asarray(inputs["gdn_conv_w"], np.float32).reshape(DEPTH, 4, 18, 128).transpose(0, 3, 2, 1))
    shared["ssd_conv_w"] = f(np.asarray(inputs["ssd_conv_w"], np.float32).reshape(DEPTH, 4, 10, 128).transpose(0, 3, 2, 1))
    shared["ssd_conv_b"] = pc(inputs["ssd_conv_b"], 10)
    shared["gdn_norm_w"] = f(np.asarray(inputs["gdn_norm_w"], np.float32).reshape(DEPTH, 128, 1))
    shared["ssd_norm_w"] = pc(inputs["ssd_norm_w"], 6)
    shared["pool_scale"] = pc(inputs["pool_scale"], 4)
    shared["pool_w"] = f(np.asarray(inputs["pool_w"], np.float32).transpose(0, 2, 1, 3))
    for k, v in consts.items():
        shared["c_" + k] = f(v)
    xp = np.asarray(inputs["x_prompt"], np.float32)
    xs = np.asarray(inputs["x_sample"], np.float32)
    for c in range(8):
        m = dict(shared)
        sl = slice(c * NSEQ, (c + 1) * NSEQ)
        m["x_prompt"] = f(xp[c % 4])
        m["x_sample"] = f(xs[sl].reshape(SW, D))
        m["state_gdn"] = f(np.asarray(inputs["state_gdn"])[:, sl])
        m["state_gdn_conv"] = f(np.asarray(inputs["state_gdn_conv"])[:, sl].reshape(DEPTH, NSEQ * 3, C_GDN))
        m["state_ssd"] = f(np.asarray(inputs["state_ssd"])[:, sl])
        m["state_ssd_conv"] = f(np.asarray(inputs["state_ssd_conv"])[:, sl].reshape(DEPTH, NSEQ * 3, C_SSD))
        m["state_pool"] = f(np.asarray(inputs["state_pool"])[:, sl])
        maps.append(m)
    return maps


def assemble(results):
    r = results
    y_prompt = np.stack([r[c]["y_prompt"] for c in range(4)])
    y_sample = np.concatenate([r[c]["y_sample"].reshape(NSEQ, DTK, D) for c in range(8)], axis=0)
    ngp = np.stack([r[c]["new_gdn_p"] for c in range(4)], axis=1)
    ngcp = np.stack([r[c]["new_gdn_conv_p"] for c in range(4)], axis=1)
    nsp = np.stack([r[c]["new_ssd_p"].reshape(DEPTH, 12, 64, 128) for c in range(4)], axis=1)
    nscp = np.stack([r[c]["new_ssd_conv_p"] for c in range(4)], axis=1)
    npp = np.stack([r[c]["new_pool_p"] for c in range(4)], axis=1)
    ngs = np.concatenate([r[c]["new_gdn_s"] for c in range(8)], axis=1)
    ngcs = np.concatenate([r[c]["new_gdn_conv_s"].reshape(DEPTH, NSEQ, 3, C_GDN) for c in range(8)], axis=1)
    nss = np.concatenate([r[c]["new_ssd_s"].reshape(DEPTH, NSEQ, 12, 64, 128) for c in range(8)], axis=1)
    nscs = np.concatenate([r[c]["new_ssd_conv_s"].reshape(DEPTH, NSEQ, 3, C_SSD) for c in range(8)], axis=1)
    nps = np.concatenate([r[c]["new_pool_s"] for c in range(8)], axis=1)
    outs = (y_prompt, y_sample, ngp, ngcp, nsp, nscp, npp, ngs, ngcs, nss, nscs, nps)
    return tuple(np.ascontiguousarray(o.astype(np.float32)) for o in outs)


def kernel(**inputs):
    if "nc" not in _NC_CACHE:
        _NC_CACHE["nc"] = build()
    nc = _NC_CACHE["nc"]
    maps = make_in_maps(inputs)
    res = run_bass_kernel_spmd(nc, maps, core_ids=list(range(8)))
    return assemble(res.results)
```

```python
import contextlib
import numpy as np
import concourse.bass as bass
import concourse.mybir as mybir
from concourse.bass_utils import run_bass_kernel_spmd

F32 = mybir.dt.float32
F32R = mybir.dt.float32r
BF16 = mybir.dt.bfloat16
AF = mybir.ActivationFunctionType
ALU = mybir.AluOpType

D = 2048
DEPTH = 4
SEQ = 2048
NSEQ = 16
DTK = 4
SW = NSEQ * DTK
DIN = 6168
C_GDN = 2304
C_SSD = 1280
EPS = 1e-6
BIG = 30000.0
OFF_QKV, OFF_ZG, OFF_B, OFF_A, OFF_ZS, OFF_XBC, OFF_DT, OFF_UP, OFF_ZP = 0, 2304, 3072, 3078, 3084, 3852, 5132, 5144, 5656
BLK_COLS = ([OFF_QKV + i * 128 for i in range(18)] + [OFF_ZG + i * 128 for i in range(6)] + [OFF_ZS + i * 128 for i in range(6)]
            + [OFF_XBC + i * 128 for i in range(10)] + [OFF_UP + i * 128 for i in range(4)] + [OFF_ZP + i * 128 for i in range(4)])
BLK_OF = {c: i for i, c in enumerate(BLK_COLS)}
NBLK = len(BLK_COLS)
PT = 512
NPT = PT // 128
NPASS = SEQ // PT
SWP = 128
NSQP = SWP // DTK
TOKMAX = PT + SWP
NTM = NPT + 1
DBG_TI = 0
SELF_SYNC_SKIP = ()


class Buf:
    def __init__(self, name, t):
        self.name = name
        self.t = t
        self.w = None
        self.r = {}
        self.gw = []
        self.gr = []
        self.excl = False

    def __getitem__(self, k):
        return self.t[k]


class Group:
    def __init__(self, name, sem):
        self.name = name
        self.sem = sem
        self.total = 0
        self.waited = False


class Sched:
    ENGS = ("pe", "act", "dve", "pool", "sp")

    def __init__(self, nc, stack):
        self.nc = nc
        self.stack = stack
        self.eng = {"pe": nc.tensor, "act": nc.scalar, "dve": nc.vector,
                    "pool": nc.gpsimd, "sp": nc.sync}
        self.sem = {e: stack.enter_context(nc.semaphore("s_" + e)) for e in self.ENGS}
        self.cnt = {e: 0 for e in self.ENGS}
        self.seen = {e: {} for e in self.ENGS}
        self.groups = []
        self.same_engine_sync = True
        self.nwait = 0

    def sb(self, name, shape, dt, stack=None):
        self.nalloc = getattr(self, "nalloc", 0) + 1
        uname = "%s_%d" % (name, self.nalloc)
        return Buf(uname, (stack or self.stack).enter_context(self.nc.sbuf_tensor(uname, list(shape), dt)))

    def ps(self, name, shape, dt):
        b = Buf(name, self.stack.enter_context(self.nc.psum_tensor(name, list(shape), dt)))
        b.excl = True
        return b

    def dram(self, name, shape, dt, kind):
        return Buf(name, self.nc.dram_tensor(name, list(shape), dt, kind=kind).ap())

    def group(self, name):
        g = Group(name, self.stack.enter_context(self.nc.semaphore("g_" + name)))
        self.groups.append(g)
        return g

    def _need(self, e, key, sem, val):
        if val <= 0 or self.seen[e].get(key, 0) >= val:
            return
        self.eng[e].wait_ge(sem, val)
        self.nwait += 1
        self.seen[e][key] = val

    def _wait_eng(self, e, dep):
        if dep is None:
            return
        f, c = dep
        if f == e and (e == "pe" or not self.same_engine_sync or e in SELF_SYNC_SKIP):
            return
        self._need(e, f, self.sem[f], c)

    def _wait_grp(self, e, g):
        if g.total > 0:
            self._need(e, g.name, g.sem, g.total)
            g.waited = True

    def _deps(self, e, reads, writes):
        for b in reads:
            self._wait_eng(e, b.w)
            if b.excl:
                for f, c in b.r.items():
                    self._wait_eng(e, (f, c))
            for g in b.gw:
                self._wait_grp(e, g)
        for b in writes:
            self._wait_eng(e, b.w)
            for f, c in b.r.items():
                self._wait_eng(e, (f, c))
            for g in b.gw:
                self._wait_grp(e, g)
            for g in b.gr:
                self._wait_grp(e, g)

    def op(self, e, fn, R=(), W=()):
        self._deps(e, R, W)
        ins = fn(self.eng[e])
        self.cnt[e] += 1
        c = self.cnt[e]
        ins.then_inc(self.sem[e], 1)
        for b in R:
            b.r[e] = c
        for b in W:
            b.w = (e, c)
            b.r = {}
            b.gw = []
            b.gr = []
        return ins

    def dma(self, q, g, out_ap, in_ap, src, dst, **kw):
        self._deps(q, [src], [dst])
        if g.waited:
            self._wait_grp(q, g)
            g.waited = False
        ins = self.eng[q].dma_start(out=out_ap, in_=in_ap, **kw)
        ins.then_inc(g.sem, 16)
        g.total += 16
        if g not in src.gr:
            src.gr.append(g)
        dst.w = None
        dst.r = {}
        if g not in dst.gw:
            dst.gw.append(g)
        return ins

    def mark(self, name):
        if not hasattr(self, "marks"):
            self.marks = []
        self.marks.append((name, dict(self.cnt)))

    def barrier(self):
        for e in self.ENGS:
            for g in self.groups:
                self._wait_grp(e, g)
            for f in self.ENGS:
                if f != e and self.cnt[f] > 0:
                    self._need(e, f, self.sem[f], self.cnt[f])

    def finish(self, e="sp"):
        for g in self.groups:
            self._wait_grp(e, g)
        for f in self.ENGS:
            if f != e and self.cnt[f] > 0:
                self._need(e, f, self.sem[f], self.cnt[f])

    def mm(self, out, lhsT, rhs, R, W, start=True, stop=True):
        return self.op("pe", lambda e: e.matmul(out, lhsT=lhsT, rhs=rhs, start=start, stop=stop), R, W)

    def act(self, out, in_, func, R, W, bias=None, scale=None, accum_out=None):
        kw = {}
        if bias is not None:
            kw["bias"] = bias
        if scale is not None:
            kw["scale"] = scale
        if accum_out is not None:
            kw["accum_out"] = accum_out
        return self.op("act", lambda e: e.activation(out, in_, func, **kw), R, W)

    def tt(self, eng, out, in0, in1, op, R, W):
        return self.op(eng, lambda e: e.tensor_tensor(out=out, in0=in0, in1=in1, op=op), R, W)

    def stt(self, eng, out, in0, scalar, in1, op0, op1, R, W):
        return self.op(eng, lambda e: e.scalar_tensor_tensor(out=out, in0=in0, scalar=scalar, in1=in1, op0=op0, op1=op1), R, W)

    def ts(self, eng, out, in0, s1, op0, R, W):
        return self.op(eng, lambda e: e.tensor_scalar(out=out, in0=in0, scalar1=s1, scalar2=None, op0=op0), R, W)

    def cp(self, eng, out, in_, R, W):
        if eng == "act":
            return self.op("act", lambda e: e.activation(out, in_, AF.Copy), R, W)
        return self.op(eng, lambda e: e.tensor_copy(out, in_), R, W)

    def ms(self, eng, ap, val, W):
        return self.op(eng, lambda e: e.memset(ap, val), (), W)


def make_consts():
    c = {}
    i = np.arange(128)
    c["ident"] = np.eye(128, dtype=np.float32)
    c["ones"] = np.ones((128, 128), np.float32)
    c["U_p"] = (i[:, None] <= i[None, :]).astype(np.float32)
    c["NEG_p"] = np.where(i[None, :] < i[:, None], -BIG, 0.0).astype(np.float32)
    c["STR_p"] = (i[None, :] > i[:, None]).astype(np.float32)
    c["SEQ_p"] = np.ones((128, 2), np.float32)
    j = np.arange(128)
    sid = j // DTK
    valid = j < SW
    same = (sid[:, None] == sid[None, :]) & valid[:, None] & valid[None, :]
    c["U_s"] = (same & (j[:, None] <= j[None, :])).astype(np.float32)
    c["NEG_s"] = np.where(same & (j[None, :] >= j[:, None]), 0.0, -BIG).astype(np.float32)
    c["STR_s"] = (same & (j[None, :] > j[:, None])).astype(np.float32)
    sm = np.zeros((128, NSEQ), np.float32)
    sm[j[valid], sid[valid]] = 1.0
    c["SEQ_s"] = sm
    c["SAME_s"] = same.astype(np.float32)
    smask = np.zeros((128, NSEQ, SWP), np.float32)
    for s in range(NSEQ):
        smask[:, s, s * DTK:(s + 1) * DTK] = 1.0
    c["SMASK"] = smask
    wins = (2, 4, 8, 16)
    bc = np.zeros((4, 128, 128), np.float32)
    b0 = np.zeros((4, 128, 128), np.float32)
    bp = np.zeros((4, 128, 128), np.float32)
    bs = np.zeros((4, 128, 128), np.float32)
    bh = np.zeros((4, 2, 128, 128), np.float32)
    for g, w in enumerate(wins):
        for t in range(128):
            for k in range(w):
                tp = t - k
                if tp >= 0:
                    bc[g, tp, t] += 1.0 / w
                    b0[g, tp, t] += 1.0 / min(w, t + 1)
                else:
                    bp[g, 128 + tp, t] += 1.0 / w
            bc[g, t, t] -= 1.0
            b0[g, t, t] -= 1.0
        for s in range(NSEQ):
            for t in range(DTK):
                col = s * DTK + t
                for k in range(w):
                    tp = t - k
                    if tp >= 0:
                        bs[g, s * DTK + tp, col] += 1.0 / w
                    else:
                        r = 15 + tp
                        bh[g, s // 8, (s % 8) * 15 + r, col] += 1.0 / w
                bs[g, col, col] -= 1.0
    c["BC"] = bc.transpose(1, 0, 2).copy()
    c["B0"] = b0.transpose(1, 0, 2).copy()
    c["BP"] = bp.transpose(1, 0, 2).copy()
    c["BS"] = bs.transpose(1, 0, 2).copy()
    c["BH"] = bh.transpose(2, 0, 1, 3).reshape(128, 8, 128).copy()
    return c


CONST_SHAPES = {"ident": [128, 128], "ones": [128, 128], "U_p": [128, 128], "NEG_p": [128, 128], "STR_p": [128, 128],
                "SEQ_p": [128, 2], "U_s": [128, 128], "NEG_s": [128, 128], "STR_s": [128, 128], "SEQ_s": [128, NSEQ],
                "SAME_s": [128, 128], "SMASK": [128, NSEQ, SWP], "BC": [128, 4, 128], "B0": [128, 4, 128],
                "BP": [128, 4, 128], "BS": [128, 4, 128], "BH": [128, 8, 128]}
R_CONSTS = ("ident", "ones", "U_p", "NEG_p", "U_s", "NEG_s", "SAME_s")
F_CONSTS = ("STR_p", "STR_s", "SEQ_p", "SEQ_s")
P_CONSTS = ("BC", "B0", "BP", "BS", "BH")


def build(npass=NPASS, depth=DEPTH, parts=("gdn", "ssd", "pool"), debug=False):
    nc = bass.Bass("TRN2", target_bir_lowering=False)
    st = contextlib.ExitStack()
    with st:
        S = Sched(nc, st)
        _build(nc, S, npass, depth, parts, debug)
        S.finish()
        nc._marks = getattr(S, "marks", [])
        nc._cnt = dict(S.cnt)
        nc._nwait = S.nwait
    return nc


def _build(nc, S, npass, depth, parts, debug):
    din = {}

    def inp(name, shape):
        din[name] = S.dram(name, shape, F32, "ExternalInput")
        return din[name]

    xp_d = inp("x_prompt", [D, SEQ])
    xs_d = inp("x_sample", [D, SW])
    sg_d = inp("state_gdn", [DEPTH, NSEQ, 6, 128, 128])
    sgc_d = inp("state_gdn_conv", [DEPTH, NSEQ * 3, C_GDN])
    ss_d = inp("state_ssd", [DEPTH, NSEQ, 12, 64, 128])
    ssc_d = inp("state_ssd_conv", [DEPTH, NSEQ * 3, C_SSD])
    sp_d = inp("state_pool", [DEPTH, NSEQ, 15, 512])
    normw_d = inp("norm_w", [DEPTH, 128, 16])
    win_d = inp("w_in", [DEPTH, NBLK, 128, 16, 128])
    wsm_d = inp("w_small", [DEPTH, 128, 16, 24])
    gcw_d = inp("gdn_conv_w", [DEPTH, 128, 18, 4])
    galog_d = inp("gdn_a_log", [DEPTH, 6])
    gdtb_d = inp("gdn_dt_bias", [DEPTH, 6])
    gnw_d = inp("gdn_norm_w", [DEPTH, 128, 1])
    scw_d = inp("ssd_conv_w", [DEPTH, 128, 10, 4])
    scb_d = inp("ssd_conv_b", [DEPTH, 128, 10])
    salog_d = inp("ssd_a_log", [DEPTH, 12])
    sdtb_d = inp("ssd_dt_bias", [DEPTH, 12])
    sd_d = inp("ssd_d", [DEPTH, 12])
    snw_d = inp("ssd_norm_w", [DEPTH, 128, 6])
    pw_d = inp("pool_w", [DEPTH, 128, 4, 128])
    psc_d = inp("pool_scale", [DEPTH, 128, 4])
    wout_d = inp("w_out", [DEPTH, 16, 128, 16, 128])
    fnw_d = inp("final_norm_w", [128, 16])
    cd = {k: inp("c_" + k, shp) for k, shp in CONST_SHAPES.items()}

    def outp(name, shape):
        return S.dram(name, shape, F32, "ExternalOutput")

    yp_d = outp("y_prompt", [D, SEQ])
    ys_d = outp("y_sample", [D, SW])
    ngp_d = outp("new_gdn_p", [DEPTH, 6, 128, 128])
    ngcp_d = outp("new_gdn_conv_p", [DEPTH, 3, C_GDN])
    nsp_d = outp("new_ssd_p", [DEPTH, 6, 128, 128])
    nscp_d = outp("new_ssd_conv_p", [DEPTH, 3, C_SSD])
    npp_d = outp("new_pool_p", [DEPTH, 15, 512])
    ngs_d = outp("new_gdn_s", [DEPTH, NSEQ, 6, 128, 128])
    ngcs_d = outp("new_gdn_conv_s", [DEPTH, NSEQ * 3, C_GDN])
    nss_d = outp("new_ssd_s", [DEPTH, NSEQ, 6, 128, 128])
    nscs_d = outp("new_ssd_conv_s", [DEPTH, NSEQ * 3, C_SSD])
    nps_d = outp("new_pool_s", [DEPTH, NSEQ, 15, 512])
    dbg_d = outp("dbg", [128, 8192]) if debug else None
    car_g = S.dram("car_g", [DEPTH, 6, 128, 128], F32, "Internal")
    car_gc = S.dram("car_gc", [DEPTH, 18, 128, 3], F32, "Internal")
    car_s = S.dram("car_s", [DEPTH, 6, 128, 128], F32, "Internal")
    car_sc = S.dram("car_sc", [DEPTH, 10, 128, 3], F32, "Internal")
    car_p = S.dram("car_p", [DEPTH, 4, 128, 128], F32, "Internal")

    xT = S.sb("xT", [128, 16, TOKMAX], F32)
    hT = S.sb("hT", [128, 16, TOKMAX], BF16)
    ycat = S.sb("ycat", [128, 16, TOKMAX], BF16)
    cr = {k: S.sb("r_" + k, CONST_SHAPES[k], F32R) for k in R_CONSTS}
    cf = {k: S.sb("f_" + k, CONST_SHAPES[k], F32) for k in F_CONSTS}
    ident_b = S.sb("ident_b", [128, 128], BF16)
    cf["SMASK"] = S.sb("f_SMASK", CONST_SHAPES["SMASK"], BF16)
    epsc = S.sb("epsc", [128, 1], F32)
    onec = S.sb("onec", [128, 1], F32)
    NWB = 4
    wb = [S.sb("wb%d" % i, [128, 16, 128], BF16) for i in range(NWB)]
    gwb = [S.group("wb%d" % i) for i in range(NWB)]
    banks = [S.ps("bank%d" % i, [128, 512], F32) for i in range(8)]
    gmisc = S.group("misc")
    gwsm = S.group("wsm")
    gout = S.group("out")
    gcar = S.group("carry")
    gstate = S.group("state")
    scr8 = S.sb("scr8", [128, 2048], F32)
    rstd = S.sb("rstd", [128, 512], F32)
    normw = S.sb("normw", [128, 16], F32)
    gcw = S.sb("gcw", [128, 18, 4], F32)
    scw = S.sb("scw", [128, 10, 4], F32)
    scb = S.sb("scb", [128, 10], F32)
    gnw = S.sb("gnw", [128, 1], F32)
    snw = S.sb("snw", [128, 6], F32)
    psc = S.sb("psc", [128, 4], F32)
    rowc = S.sb("rowc", [128, 48], F32)
    negA = S.sb("negA", [128, 18], F32)
    fnw = S.sb("fnw", [128, 16], F32)
    wsm = S.sb("wsm", [128, 16, 24], BF16)
    diag = S.sb("diag", [128, 4, 128], F32R)
    small = S.sb("small", [128, NTM, 24], F32)
    GD = S.sb("GD", [128, NTM, 18], F32R)
    negGD = S.sb("negGD", [128, NTM, 18], F32R)
    cum = S.sb("cum", [128, NTM, 18], F32)
    EX1 = S.sb("EX1", [128, NTM, 18], F32)
    EX2 = S.sb("EX2", [128, NTM, 18], F32)
    EX3 = S.sb("EX3", [128, NTM, 18], F32)
    tot = S.sb("tot", [128, NTM, 18], F32)
    beta = S.sb("beta", [128, NTM, 6], F32)
    nbeta = S.sb("nbeta", [128, NTM, 6], F32)
    dtv = S.sb("dtv", [128, NTM, 12], F32)
    glbc = S.sb("glbc", [128, NSEQ, 18], F32)
    gdm = S.sb("gdm", [128, NSEQ, 18], F32)
    tmp18 = S.sb("tmp18", [128, 18], F32)
    tmp18b = S.sb("tmp18b", [128, 18], F32)
    ctok_p = S.sb("ctok_p", [3, 128], F32)
    ctok_s = S.sb("ctok_s", [NSEQ * 3, 128], F32)
    hst_tok = S.sb("hst_tok", [NSEQ * 3, 128], F32)
    pre = S.sb("pre", [128, 4 + PT], F32R)
    pre_s = S.sb("pre_s", [128, NSQP, 8], F32R)
    hal32 = S.sb("hal32", [128, 3], F32)
    tail32 = S.sb("tail32", [128, 3], F32)
    tails32 = S.sb("tails32", [128, NSEQ, 3], F32)

    sqb = S.sb("sqb", [128, 2048], F32R)
    sq = sqb.t[:, :].rearrange("p (c n) -> p c n", c=16)
    xtok = scr8.t

    def I32v(k):
        return cr[k].t.bitcast(F32)

    for k in R_CONSTS:
        S.dma("sp", gmisc, scr8[:, 0:128], cd[k][:], cd[k], scr8)
        S.cp("dve", cr[k][:], scr8[:, 0:128], [scr8], [cr[k]])
    for k in F_CONSTS:
        S.dma("sp", gmisc, cf[k][:], cd[k][:], cd[k], cf[k])
    S.cp("dve", ident_b[:], I32v("ident")[:], [cr["ident"]], [ident_b])
    S.dma("sp", gmisc, scr8[:, :], cd["SMASK"][:].rearrange("p a b -> p (a b)"), cd["SMASK"], scr8)
    S.cp("dve", cf["SMASK"][:].rearrange("p a b -> p (a b)"), scr8[:, :], [scr8], [cf["SMASK"]])
    S.ms("pool", epsc[:], EPS, [epsc])
    S.ms("pool", onec[:], 1.0, [onec])
    zero32 = S.sb("zero32", [128, 256], F32)
    S.ms("pool", zero32[:, :], 0.0, [zero32])
    S.cp("pool", pre_s[:, :, :], zero32[:, :].rearrange("p (s t) -> p s t", s=NSQP), [zero32], [pre_s])
    S.dma("sp", gmisc, fnw[:], fnw_d.t[:, :], fnw_d, fnw)

    bank_rr = [0]

    def dense_bank():
        bank_rr[0] ^= 1
        return banks[bank_rr[0]]

    wide_rr = [0]

    def wide_bank():
        wide_rr[0] = (wide_rr[0] + 1) % 8
        return banks[wide_rr[0]]

    wstate = {"n": 0}

    def load_block(src_ap, srcbuf):
        i = wstate["n"] % NWB
        wstate["n"] += 1
        S.dma("pool", gwb[i], wb[i][:], src_ap, srcbuf, wb[i])
        return wb[i]

    def win_ap(l, col0):
        return win_d.t[l, BLK_OF[col0]]

    def wout_ap(l, col0):
        return wout_d.t[l, col0 // 128]

    def rsq_from(dst, src, scale, R, W):
        S.act(dst, src, AF.Ln, R, W, bias=epsc[:, 0:1], scale=scale)
        S.act(dst, dst, AF.Exp, W, W, scale=-0.5)

    def dbg_dump(ap_sb, buf, col0, ncol, rows=128):
        if dbg_d is not None:
            S.dma("sp", gout, dbg_d.t[:rows, col0:col0 + ncol], ap_sb, buf, dbg_d)

    def load_x(src_d, row0, W, col0):
        S.dma("sp", gmisc, xtok[:W, :], src_d.t[row0:row0 + W, :], src_d, scr8)
        for c in range(16):
            b = dense_bank()
            S.mm(b[:, :W], xtok[:W, c * 128:(c + 1) * 128], I32v("ident")[:W, :W], [scr8, cr["ident"]], [b])
            S.cp("act" if c % 2 else "dve", xT[:, c, col0:col0 + W], b[:, :W], [b], [xT])

    def phase_a(tiles):
        for (c0, W, _) in tiles:
            S.act(sq[:, :, :W], xT[:, :, c0:c0 + W], AF.Square, [xT], [sqb])
            b = dense_bank()
            for c in range(16):
                S.mm(b[:, :W], cr["ones"][:], sq[:, c, :W], [cr["ones"], sqb], [b], start=(c == 0), stop=(c == 15))
            rsq_from(rstd[:, :W], b[:, :W], 1.0 / D, [b], [rstd])
            S.tt("dve", xtok[:, :16 * W].rearrange("p (c n) -> p c n", c=16), xT[:, :, c0:c0 + W],
                 rstd[:, :W].unsqueeze(1).to_broadcast([128, 16, W]), ALU.mult, [xT, rstd], [scr8])
            S.tt("dve", hT[:, :, c0:c0 + W], xtok[:, :16 * W].rearrange("p (c n) -> p c n", c=16),
                 normw[:].unsqueeze(2).to_broadcast([128, 16, W]), ALU.mult, [scr8, normw], [hT])

    def final_out(tiles, ps_i):
        for (c0, W, mode) in tiles:
            S.act(sq[:, :, :W], xT[:, :, c0:c0 + W], AF.Square, [xT], [sqb])
            b = dense_bank()
            for c in range(16):
                S.mm(b[:, :W], cr["ones"][:], sq[:, c, :W], [cr["ones"], sqb], [b], start=(c == 0), stop=(c == 15))
            rsq_from(rstd[:, :W], b[:, :W], 1.0 / D, [b], [rstd])
            yn = hT.t.bitcast(F32)
            ynv = yn[:, :, 0:W]
            S.tt("dve", ynv, xT[:, :, c0:c0 + W], rstd[:, :W].unsqueeze(1).to_broadcast([128, 16, W]), ALU.mult, [xT, rstd], [hT])
            S.tt("dve", ynv, ynv, fnw[:].unsqueeze(2).to_broadcast([128, 16, W]), ALU.mult, [hT, fnw], [hT])
            if mode == "p":
                S.dma("sp", gout, yp_d.t.rearrange("(c p) t -> p c t", p=128)[:, :, ps_i * PT + c0:ps_i * PT + c0 + W], ynv, hT, yp_d)
            else:
                S.dma("sp", gout, ys_d.t.rearrange("(c p) t -> p c t", p=128), yn[:, :, 0:SW], hT, ys_d)

    def drain(gen):
        for _ in gen:
            pass

    def interleave(ga, gb):
        a_done = ga is None
        b_done = gb is None
        while not (a_done and b_done):
            if not a_done:
                try:
                    next(ga)
                except StopIteration:
                    a_done = True
            if not b_done:
                try:
                    next(gb)
                except StopIteration:
                    b_done = True

    def proj_block(wbuf, cgs, evac):
        for (c0, n) in cgs:
            b = dense_bank()
            for c in range(16):
                S.mm(b[:, :n], wbuf[:, c, :], hT[:, c, c0:c0 + n], [wbuf, hT], [b], start=(c == 0), stop=(c == 15))
            evac(c0, n, b)
            yield

    def conv_block(l, wbuf, cgs, ps_i, cw, cwi, bias_ap, dest_fn, car_d, car_i, hist_src, out_s, out_p, last_pass):
        for j in range(4):
            S.ts("dve", diag[:, j, :], I32v("ident")[:, :], cw[:, cwi, j:j + 1], ALU.mult, [cr["ident"], cw], [diag])
        if ps_i == 0:
            S.cp("dve", pre[:, 0:3], zero32[:, 0:3], [zero32], [pre])
        else:
            S.dma("sp", gcar, hal32[:], car_d.t[l, car_i], car_d, hal32)
            S.cp("dve", pre[:, 0:3], hal32[:], [hal32], [pre])
        has_s = any(c0 >= PT for (c0, _) in cgs)
        if has_s:
            b = dense_bank()
            S.dma("sp", gstate, hst_tok[:, :], hist_src[0], hist_src[1], hst_tok)
            S.mm(b[:, 0:NSEQ * 3], hst_tok[:, :], I32v("ident")[:NSEQ * 3, :NSEQ * 3], [hst_tok, cr["ident"]], [b])
            S.cp("act", pre_s[:, 0:NSEQ, 0:3], b[:, 0:NSEQ * 3].rearrange("p (s r) -> p s r", s=NSEQ), [b], [pre_s])

        def evac(c0, n, b):
            if c0 >= PT:
                S.cp("act", pre_s[:, :, 3:7], b[:, :SWP].rearrange("p (s t) -> p s t", s=NSQP), [b], [pre_s])
                S.cp("dve", tails32[:, :, :], b[:, :SW].rearrange("p (s t) -> p s t", s=NSEQ)[:, :, 1:4], [b], [tails32])
            else:
                S.cp("act", pre[:, 3 + c0:3 + c0 + n], b[:, :n], [b], [pre])
                if c0 + n == PT:
                    S.cp("dve", tail32[:, :], b[:, n - 3:n], [b], [tail32])
        yield from proj_block(wbuf, cgs, evac)
        if ps_i < last_pass:
            S.dma("sp", gcar, car_d.t[l, car_i], tail32[:], tail32, car_d)
        else:
            b = dense_bank()
            S.mm(b[:3, 128:256], tail32[:, :], I32v("ident")[:, :], [tail32, cr["ident"]], [b])
            S.cp("dve", ctok_p[:, :], b[:3, 128:256], [b], [ctok_p])
            S.dma("sp", gout, out_p[0], ctok_p[:, :], ctok_p, out_p[1])
        if has_s:
            b = dense_bank()
            S.mm(b[:NSEQ * 3, 256:384], tails32[:, :, :].rearrange("p s r -> p (s r)"), I32v("ident")[:, :], [tails32, cr["ident"]], [b])
            S.cp("dve", ctok_s[:, :], b[:NSEQ * 3, 256:384], [b], [ctok_s])
            S.dma("sp", gout, out_s[0], ctok_s[:, :], ctok_s, out_s[1])
        for (c0, n) in cgs:
            b = dense_bank()
            if c0 >= PT:
                for j in range(4):
                    S.mm(b[:, :SWP].rearrange("p (s t) -> p s t", s=NSQP), diag[:, j, :], pre_s[:, :, j:j + 4], [diag, pre_s], [b],
                         start=(j == 0), stop=(j == 3))
            else:
                for j in range(4):
                    S.mm(b[:, :n], diag[:, j, :], pre[:, c0 + j:c0 + j + n], [diag, pre], [b], start=(j == 0), stop=(j == 3))
            dap, dbuf = dest_fn(c0, n)
            if bias_ap is None:
                S.act(dap, b[:, :n], AF.Silu, [b], [dbuf])
            else:
                S.act(dap, b[:, :n], AF.Silu, [b, scb], [dbuf], bias=bias_ap)
            yield

    def gdn_phase(l, ps_i, tiles, cgs, last_pass):
        NT = len(tiles)
        with contextlib.ExitStack() as ph:
            def A(name, shape, dt):
                return S.sb(name, shape, dt, stack=ph)
            HB = [dict(qkn=A("qkn%d" % i, [128, 2, TOKMAX], F32R), vr=A("vr%d" % i, [128, TOKMAX], F32R),
                       zs=A("zs%d" % i, [128, TOKMAX], F32)) for i in range(2)]
            Xs = [A("Xs%d" % i, [128, 128], F32R) for i in range(NTM)]
            Es = [A("Es%d" % i, [128, 128], F32) for i in range(NTM)]
            QT = [A("QT%d" % i, [128, 384], F32R) for i in range(NTM)]
            Tfin = A("Tfin", [128, NTM, 128], F32R)
            QKT = A("QKT", [128, NTM, 128], F32R)
            kg = A("kg", [128, NTM, 128], F32R)
            kdec = A("kdec", [128, NTM, 128], F32R)
            vtok = A("vtok", [128, NTM, 128], F32R)
            ub = A("ub", [128, NTM, 128], F32)
            wT = A("wT", [128, NTM, 128], F32R)
            Sst = A("Sst", [128, 128], F32)
            Sr = A("Sr", [128, 128], F32R)
            vnew = A("vnew", [128, 128], F32R)
            qse = Es[0]
            otoks = [A("otok%d" % i, [128, 128], F32) for i in range(2)]
            onr = A("onr", [128, 128], F32R)
            ssq1 = A("ssq1", [128, 1], F32)
            Ssm = A("Ssm", [128, NSEQ, 128], F32)
            Ssr = A("Ssr", [128, 8, 128], F32R)
            wqm = A("wqm", [128, 8, 128], F32R)
            kdm = wqm
            GD32 = GD.t.bitcast(F32)
            sq2 = sqb.t[:, :].rearrange("p (a n) -> p a n", a=4)
            bs = banks[2]

            def proj_gen(h):
                qkn, vr, zs = HB[h % 2]["qkn"], HB[h % 2]["vr"], HB[h % 2]["zs"]
                qkn32 = qkn.t.bitcast(F32)
                for which in range(3):
                    cidx = which * 6 + h
                    col = OFF_QKV + cidx * 128
                    wbuf = load_block(win_ap(l, col), win_d)
                    if which < 2:
                        dfn = (lambda c0, n, which=which: (qkn[:, which, c0:c0 + n], qkn))
                    else:
                        dfn = (lambda c0, n: (vr[:, c0:c0 + n], vr))
                    csl = slice(cidx * 128, (cidx + 1) * 128)
                    yield from conv_block(l, wbuf, cgs, ps_i, gcw, cidx, None, dfn, car_gc, cidx, (sgc_d.t[l, :, csl], sgc_d),
                                          (ngcs_d.t[l, :, csl], ngcs_d), (ngcp_d.t[l, :, csl], ngcp_d), last_pass)
                wbuf = load_block(win_ap(l, OFF_ZG + h * 128), win_d)
                yield from proj_block(wbuf, cgs, lambda c0, n, b: S.act(zs[:, c0:c0 + n], b[:, :n], AF.Silu, [b], [zs]))
                for (c0, n) in cgs:
                    S.act(sq2[:, 0:2, :n], qkn32[:, :, c0:c0 + n], AF.Square, [qkn], [sqb])
                    b1 = dense_bank()
                    S.mm(b1[:, :n], cr["ones"][:], sq2[:, 0, :n], [cr["ones"], sqb], [b1])
                    b2 = dense_bank()
                    S.mm(b2[:, :n], cr["ones"][:], sq2[:, 1, :n], [cr["ones"], sqb], [b2])
                    rsq_from(rstd[:, :n], b1[:, :n], 1.0, [b1], [rstd])
                    S.stt("dve", qkn[:, 0, c0:c0 + n], qkn32[:, 0, c0:c0 + n], 128.0 ** -0.5, rstd[:, :n], ALU.mult, ALU.mult,
                          [qkn, rstd], [qkn])
                    rsq_from(rstd[:, :n], b2[:, :n], 1.0, [b2], [rstd])
                    S.tt("dve", qkn[:, 1, c0:c0 + n], qkn32[:, 1, c0:c0 + n], rstd[:, :n], ALU.mult, [qkn, rstd], [qkn])
                    yield

            def chain_gen(h):
                qkn, vr, zs = HB[h % 2]["qkn"], HB[h % 2]["vr"], HB[h % 2]["zs"]
                def cm(ti):
                    c0, W, mode = tiles[ti]
                    if mode == "p":
                        return c0, W, cr["U_p"], cr["NEG_p"], cf["STR_p"], 6
                    return c0, W, cr["U_s"], cr["NEG_s"], cf["STR_s"], 1

                yield
                for ti in range(NT):
                    c0, W, U, NEG, STR, n_it = cm(ti)
                    S.ts("dve", Xs[ti][:W, :W], U.t.bitcast(F32)[:W, :W], GD32[:W, ti, h:h + 1], ALU.mult, [U, GD], [Xs[ti]])
                yield
                for ti in range(NT):
                    c0, W, U, NEG, STR, n_it = cm(ti)
                    bk = banks[3 + ti]
                    S.mm(bk[:W, 0:W], cr["ones"][:W, :W], Xs[ti][:W, :W], [cr["ones"], Xs[ti]], [bk], start=True, stop=False)
                    S.mm(bk[:W, 0:W], U[:W, :W], negGD[:W, ti, h:h + 1].to_broadcast([W, W]), [U, negGD], [bk], start=False, stop=False)
                    S.mm(bk[:W, 0:W], cr["ident"][:W, :W], NEG[:W, :W], [cr["ident"], NEG], [bk], start=False, stop=True)
                yield
                for ti in range(NT):
                    c0, W, U, NEG, STR, n_it = cm(ti)
                    bk = banks[3 + ti]
                    S.act(Es[ti][:W, :W], bk[:W, 0:W], AF.Exp, [bk], [Es[ti]])
                    S.tt("dve", Xs[ti][:W, :W], Es[ti][:W, :W], STR[:W, :W], ALU.mult, [Es[ti], STR], [Xs[ti]])
                yield
                for ti in range(NT):
                    c0, W, U, NEG, STR, n_it = cm(ti)
                    bk = banks[3 + ti]
                    kT = qkn[:, 1, c0:c0 + W]
                    qT = qkn[:, 0, c0:c0 + W]
                    S.mm(bk[:W, 128:384].rearrange("p (a i) -> p a i", a=2), kT, qkn[:, :, c0:c0 + W], [qkn], [bk])
                yield
                for ti in range(NT):
                    c0, W, U, NEG, STR, n_it = cm(ti)
                    bk = banks[3 + ti]
                    S.stt("dve", QT[ti][:W, 0:W], bk[:W, 256:256 + W], nbeta[:W, ti, h:h + 1], Xs[ti].t.bitcast(F32)[:W, :W], ALU.mult, ALU.mult,
                          [bk, nbeta, Xs[ti]], [QT[ti]])
                    S.tt("dve", QKT[:W, ti, :W], bk[:W, 128:128 + W], Es[ti][:W, :W], ALU.mult, [bk, Es[ti]], [QKT])
                yield
                for ti in range(NT):
                    c0, W, U, NEG, STR, n_it = cm(ti)
                    bk = banks[3 + ti]
                    S.mm(bk[:W, 384:384 + W], QT[ti][:W, 0:W], cr["ident"][:W, :W], [QT[ti], cr["ident"]], [bk])
                    S.mm(bk[:W, 0:128], qkn[:, 1, c0:c0 + W], cr["ident"][:, :], [qkn, cr["ident"]], [bk])
                    S.mm(bk[:W, 128:256], vr[:, c0:c0 + W], cr["ident"][:, :], [vr, cr["ident"]], [bk])
                yield
                for ti in range(NT):
                    c0, W, U, NEG, STR, n_it = cm(ti)
                    bk = banks[3 + ti]
                    S.cp("act", QT[ti][:W, 256:256 + W], bk[:W, 384:384 + W], [bk], [QT[ti]])
                    S.cp("dve", QT[ti][:W, 128:128 + W], I32v("ident")[:W, :W], [cr["ident"]], [QT[ti]])
                    S.act(kg[:W, ti, :], bk[:W, 0:128], AF.Copy, [bk, EX1], [kg], scale=EX1[:W, ti, h:h + 1])
                    S.act(kdec[:W, ti, :], bk[:W, 0:128], AF.Copy, [bk, EX2], [kdec], scale=EX2[:W, ti, h:h + 1])
                    S.cp("dve", vtok[:W, ti, :], bk[:W, 128:256], [bk], [vtok])
                for k in range(1, 7):
                    for ti in range(NT):
                        c0, W, U, NEG, STR, n_it = cm(ti)
                        if k > n_it:
                            continue
                        bk = banks[3 + ti]
                        S.mm(bk[:W, 0:256].rearrange("p (a i) -> p a i", a=2), QT[ti][:W, 256:256 + W],
                             QT[ti][:W, 0:256].rearrange("p (a i) -> p a i", a=2), [QT[ti]], [bk])
                        S.mm(bk[:W, 256:256 + W], QT[ti][:W, 0:W], QT[ti][:W, 256:256 + W], [QT[ti]], [bk])
                    yield
                    for ti in range(NT):
                        c0, W, U, NEG, STR, n_it = cm(ti)
                        if k > n_it:
                            continue
                        bk = banks[3 + ti]
                        S.cp("act", QT[ti][:W, :].rearrange("p (a b) -> p a b", a=3)[:, 0::2, :],
                             bk[:W, 0:384].rearrange("p (a b) -> p a b", a=3)[:, 0::2, :], [bk], [QT[ti]])
                        S.tt("dve", QT[ti][:W, 128:128 + W], QT[ti].t.bitcast(F32)[:W, 128:128 + W], bk[:W, 128:128 + W], ALU.add,
                             [QT[ti], bk], [QT[ti]])
                    yield
                yield
                for ti in range(NT):
                    c0, W, U, NEG, STR, n_it = cm(ti)
                    bk = banks[3 + ti]
                    S.mm(bk[:W, 128:128 + W], QT[ti][:W, 256:256 + W], QT[ti][:W, 128:128 + W], [QT[ti]], [bk])
                yield
                for ti in range(NT):
                    c0, W, U, NEG, STR, n_it = cm(ti)
                    bk = banks[3 + ti]
                    S.tt("dve", Tfin[:W, ti, :W], QT[ti].t.bitcast(F32)[:W, 128:128 + W], bk[:W, 128:128 + W], ALU.add, [QT[ti], bk], [Tfin])
                yield
                for ti in range(NT):
                    c0, W, U, NEG, STR, n_it = cm(ti)
                    bk = banks[3 + ti]
                    S.mm(bk[:W, 0:128], Tfin[:W, ti, :W], vtok[:W, ti, :], [Tfin, vtok], [bk])
                    S.mm(bk[:, 128:128 + W], kg[:W, ti, :], Tfin[:W, ti, :W], [kg, Tfin], [bk])
                yield
                for ti in range(NT):
                    c0, W, U, NEG, STR, n_it = cm(ti)
                    bk = banks[3 + ti]
                    S.act(ub[:W, ti, :], bk[:W, 0:128], AF.Copy, [bk, beta], [ub], scale=beta[:W, ti, h:h + 1])
                    S.cp("dve", wT[:, ti, :W], bk[:, 128:128 + W], [bk], [wT])

                if ps_i == 0:
                    S.ms("pool", Sst[:, :], 0.0, [Sst])
                else:
                    S.dma("sp", gcar, Sst[:, :], car_g.t[l, h], car_g, Sst)
                S.cp("act", Sr[:, :], Sst[:, :], [Sst], [Sr])

                def out_norm(ti):
                    c0, W, mode = tiles[ti]
                    otok = otoks[ti % 2]
                    S.act(onr[:W, :], otok[:W, :], AF.Square, [otok], [onr, ssq1], accum_out=ssq1[:W, 0:1])
                    rsq_from(ssq1[:W, :], ssq1[:W, :], 1.0 / 128, [ssq1], [ssq1])
                    S.ts("dve", onr[:W, :], otok[:W, :], ssq1[:W, 0:1], ALU.mult, [otok, ssq1], [onr])
                    bd = dense_bank()
                    S.mm(bd[:, :W], onr[:W, :], cr["ident"][:W, :W], [onr, cr["ident"]], [bd])
                    S.stt("dve", ycat[:, h, c0:c0 + W], bd[:, :W], gnw[:, 0:1], zs[:, c0:c0 + W], ALU.mult, ALU.mult,
                          [bd, gnw, zs], [ycat])

                def mid(ti, W, qs_ap=None, qs_buf=None):
                    if qs_ap is None:
                        qs_ap, qs_buf = bs[:W, 128:256], bs
                    S.stt("dve", vnew[:W, :], bs[:W, 0:128], nbeta[:W, ti, h:h + 1], ub[:W, ti, :], ALU.mult, ALU.add,
                          [bs, nbeta, ub], [vnew])
                    S.act(qse[:W, :], qs_ap, AF.Copy, [qs_buf, EX1], [qse], scale=EX1[:W, ti, h:h + 1])
                    S.mm(bs[:W, 256:384], QKT[:W, ti, :W], vnew[:W, :], [QKT, vnew], [bs])
                    S.tt("dve", otoks[ti % 2][:W, :], bs[:W, 256:384], qse[:W, :], ALU.add, [bs, qse], [otoks[ti % 2]])

                yield
                for ti in range(NT):
                    c0, W, mode = tiles[ti]
                    if mode != "p":
                        continue
                    S.mm(bs[:W, 0:128], wT[:, ti, :W], Sr[:, :], [wT, Sr], [bs])
                    S.mm(bs[:W, 128:256], qkn[:, 0, c0:c0 + W], Sr[:, :], [qkn, Sr], [bs])
                    if ti > 0:
                        out_norm(ti - 1)
                    yield
                    mid(ti, W)
                    yield
                    S.mm(bs[:, 384:512], kdec[:W, ti, :], vnew[:W, :], [kdec, vnew], [bs])
                    S.stt("dve", Sst[:, :], Sst[:, :], EX3[:, ti, h:h + 1], bs[:, 384:512], ALU.mult, ALU.add, [Sst, EX3, bs], [Sst])
                    S.cp("act", Sr[:, :], Sst[:, :], [Sst], [Sr])
                    yield
                out_norm(NPT - 1)
                if ps_i < last_pass:
                    S.dma("sp", gcar, car_g.t[l, h], Sst[:, :], Sst, car_g)
                else:
                    S.dma("sp", gout, ngp_d.t[l, h], Sst[:, :], Sst, ngp_d)

                yield
                for ti in range(NT):
                    c0, W, mode = tiles[ti]
                    if mode != "s":
                        continue
                    S.dma("sp", gstate, Ssm[:, :, :], sg_d.t[l, :, h].rearrange("s k v -> k s v"), sg_d, Ssm)
                    bq = banks[7]
                    for half in range(2):
                        sl8 = slice(half * 8, half * 8 + 8)
                        S.cp("act", Ssr[:, :, :], Ssm[:, sl8, :], [Ssm], [Ssr])
                        S.tt("dve", wqm[:, :, :], wT.t.bitcast(F32)[:, ti, :].unsqueeze(1).to_broadcast([128, 8, 128]),
                             cf["SMASK"][:, sl8, :], ALU.mult, [wT, cf["SMASK"]], [wqm])
                        for s8 in range(8):
                            S.mm(bs[:, 0:128], wqm[:, s8, :], Ssr[:, s8, :], [wqm, Ssr], [bs],
                                 start=(half == 0 and s8 == 0), stop=(half == 1 and s8 == 7))
                        S.tt("dve", wqm[:, :, :], qkn.t.bitcast(F32)[:, 0, c0:c0 + 128].unsqueeze(1).to_broadcast([128, 8, 128]),
                             cf["SMASK"][:, sl8, :], ALU.mult, [qkn, cf["SMASK"]], [wqm])
                        for s8 in range(8):
                            S.mm(bq[:, 0:128], wqm[:, s8, :], Ssr[:, s8, :], [wqm, Ssr], [bq],
                                 start=(half == 0 and s8 == 0), stop=(half == 1 and s8 == 7))
                    mid(ti, W, bq[:, 0:128], bq)
                    yield
                    for half in range(2):
                        sl8 = slice(half * 8, half * 8 + 8)
                        S.tt("dve", kdm[:, :, :], kdec.t.bitcast(F32)[:, ti, :].unsqueeze(1).to_broadcast([128, 8, 128]),
                             cf["SEQ_s"][:, sl8].unsqueeze(2).to_broadcast([128, 8, 128]), ALU.mult, [kdec, cf["SEQ_s"]], [kdm])
                        for q4 in range(2):
                            bk = banks[3 + q4]
                            for s4 in range(4):
                                S.mm(bk[:, s4 * 128:(s4 + 1) * 128], kdm[:, q4 * 4 + s4, :], vnew[:, :], [kdm, vnew], [bk])
                            sl4 = slice(half * 8 + q4 * 4, half * 8 + q4 * 4 + 4)
                            S.tt("dve", Ssm[:, sl4, :], Ssm[:, sl4, :], glbc[:, sl4, h:h + 1].to_broadcast([128, 4, 128]), ALU.mult,
                                 [Ssm, glbc], [Ssm])
                            S.tt("dve", Ssm[:, sl4, :], Ssm[:, sl4, :], bk[:, :].rearrange("p (s v) -> p s v", s=4), ALU.add, [Ssm, bk], [Ssm])
                    S.dma("sp", gout, ngs_d.t[l, :, h].rearrange("s k v -> k s v"), Ssm[:, :, :], Ssm, ngs_d)
                    out_norm(ti)
                yield
            drain(proj_gen(0))
            for h in range(6):
                interleave(chain_gen(h), proj_gen(h + 1) if h + 1 < 6 else None)
            S.barrier()

    def ssd_phase(l, ps_i, tiles, cgs, last_pass):
        NT = len(tiles)
        with contextlib.ExitStack() as ph:
            def A(name, shape, dt):
                return S.sb(name, shape, dt, stack=ph)
            xs = A("xs", [128, 3, TOKMAX], BF16)
            Bs = A("Bs", [128, TOKMAX], BF16)
            Cs = A("Cs", [128, TOKMAX], BF16)
            zss = A("zss", [128, 3, TOKMAX], F32)
            xdtP = [A("xdt%d" % i, [128, 6, 64], F32R) for i in range(2)]
            xDP = [A("xD%d" % i, [128, 6, 64], F32) for i in range(2)]
            xdtwP = [A("xdtw%d" % i, [128, 6, 64], F32R) for i in range(2)]
            BtokP = [A("Btok%d" % i, [128, 128], F32R) for i in range(2)]
            X6 = A("X6", [128, 6, 128], F32R)
            E6 = A("E6", [128, 6, 128], F32)
            MT6 = A("MT6", [128, 6, 128], F32R)
            Hst = A("Hst", [128, 3, 128], F32)
            HTb = A("HTb", [128, 384], BF16)
            yoff = A("yoff", [128, 6, 64], F32)
            ytok = A("ytok", [128, 384], F32)
            ytr = A("ytr", [128, 384], F32R)
            yg = A("yg", [128, 3, 128], F32)
            sqg = A("sqg", [128, 3, 128], F32R)
            dtaexp = A("dtaexp", [128, 6, 64], F32)
            dch = A("dch", [128, 3, 2], F32)
            dchs = A("dchs", [128, 3, NSEQ], F32)
            Hsm = A("Hsm", [128, NSEQ, 128], F32)
            HTs = A("HTs", [128, NSEQ, 128], BF16)
            Cm = A("Cm", [128, NSEQ, 128], BF16)
            xdtwm = A("xdtwm", [128, NSEQ, 128], F32R)
            GD32 = GD.t.bitcast(F32)
            b2, b3, b4, b5, b6, b7 = banks[2], banks[3], banks[4], banks[5], banks[6], banks[7]

            for g in range(2):
                blocks = [("x", 3 * g + blk, blk) for blk in range(3)] + [("B", 6 + g, 0), ("C", 8 + g, 0)]
                for (kind, cidx, blk) in blocks:
                    wbuf = load_block(win_ap(l, OFF_XBC + cidx * 128), win_d)
                    if kind == "x":
                        dfn = (lambda c0, n, blk=blk: (xs[:, blk, c0:c0 + n], xs))
                    elif kind == "B":
                        dfn = (lambda c0, n: (Bs[:, c0:c0 + n], Bs))
                    else:
                        dfn = (lambda c0, n: (Cs[:, c0:c0 + n], Cs))
                    csl = slice(cidx * 128, (cidx + 1) * 128)
                    drain(conv_block(l, wbuf, cgs, ps_i, scw, cidx, scb[:, cidx:cidx + 1], dfn, car_sc, cidx, (ssc_d.t[l, :, csl], ssc_d),
                                     (nscs_d.t[l, :, csl], nscs_d), (nscp_d.t[l, :, csl], nscp_d), last_pass))
                for blk in range(3):
                    wbuf = load_block(win_ap(l, OFF_ZS + (3 * g + blk) * 128), win_d)
                    drain(proj_block(wbuf, cgs, lambda c0, n, b, blk=blk: S.act(zss[:, blk, c0:c0 + n], b[:, :n], AF.Silu, [b], [zss])))
                for blk in range(3):
                    if ps_i == 0:
                        S.ms("pool", Hst[:, blk, :], 0.0, [Hst])
                    else:
                        S.dma("sp", gcar, Hst[:, blk, :], car_s.t[l, 3 * g + blk], car_s, Hst)
                gc0 = 6 + 6 * g

                def pre(ti):
                        c0, W, mode = tiles[ti]
                        U, NEG = (cr["U_p"], cr["NEG_p"]) if mode == "p" else (cr["U_s"], cr["NEG_s"])
                        xdt_, xD_, xdtw_, Btok_ = xdtP[ti % 2], xDP[ti % 2], xdtwP[ti % 2], BtokP[ti % 2]
                        yb = banks[ti % 2]
                        for blk in range(3):
                            S.mm(b3[:W, blk * 128:(blk + 1) * 128], xs[:, blk, c0:c0 + W], ident_b[:, :], [xs, ident_b], [b3])
                        S.mm(b3[:W, 384:512], Bs[:, c0:c0 + W], ident_b[:, :], [Bs, ident_b], [b3])
                        xv = b3[:W, 0:384].rearrange("p (h q) -> p h q", h=6)
                        S.tt("dve", xdt_[:W, :, :], xv, dtv[:W, ti, 6 * g:6 * g + 6].unsqueeze(2).to_broadcast([W, 6, 64]), ALU.mult, [b3, dtv], [xdt_])
                        S.tt("dve", xD_[:W, :, :], xv, rowc[:W, 36 + 6 * g:42 + 6 * g].unsqueeze(2).to_broadcast([W, 6, 64]), ALU.mult, [b3, rowc], [xD_])
                        S.cp("act", Btok_[:W, :], b3[:W, 384:512], [b3], [Btok_])
                        S.tt("dve", xdtw_[:W, :, :], xdt_.t.bitcast(F32)[:W, :, :], EX2[:W, ti, gc0:gc0 + 6].unsqueeze(2).to_broadcast([W, 6, 64]),
                             ALU.mult, [xdt_, EX2], [xdtw_])
                        S.mm(b4[:W, 0:W], Bs[:, c0:c0 + W], Cs[:, c0:c0 + W], [Bs, Cs], [b4])
                        S.tt("dve", X6[:W, :, :W], U.t.bitcast(F32)[:W, :W].unsqueeze(1).to_broadcast([W, 6, W]),
                             GD32[:W, ti, gc0:gc0 + 6].unsqueeze(2).to_broadcast([W, 6, W]), ALU.mult, [U, GD], [X6])
                        for (bk, h0, h1) in ((b5, 0, 4), (b6, 4, 6)):
                            nh = h1 - h0
                            ov = bk[:W, 0:nh * 128].rearrange("p (h i) -> p h i", h=nh)[:, :, :W]
                            S.mm(ov, cr["ones"][:W, :W], X6[:W, h0:h1, :W], [cr["ones"], X6], [bk], start=True, stop=False)
                            S.mm(ov, U[:W, :W], negGD[:W, ti, gc0 + h0:gc0 + h1].unsqueeze(2).to_broadcast([W, nh, W]), [U, negGD], [bk],
                                 start=False, stop=False)
                            S.mm(ov, cr["ident"][:W, :W], NEG[:W, :W].unsqueeze(1).to_broadcast([W, nh, W]), [cr["ident"], NEG], [bk],
                                 start=False, stop=True)
                            S.act(E6[:W, h0:h1, :W], ov, AF.Exp, [bk], [E6])
                        S.tt("dve", MT6[:W, :, :W], E6[:W, :, :W], b4[:W, 0:W].unsqueeze(1).to_broadcast([W, 6, W]), ALU.mult, [E6, b4], [MT6])
                        for hh in range(6):
                            S.mm(yb[:W, hh * 64:(hh + 1) * 64], MT6[:W, hh, :W], xdt_[:W, hh, :], [MT6, xdt_], [yb])

                def post(ti):
                        c0, W, mode = tiles[ti]
                        U, NEG = (cr["U_p"], cr["NEG_p"]) if mode == "p" else (cr["U_s"], cr["NEG_s"])
                        xdt_, xD_, xdtw_, Btok_ = xdtP[ti % 2], xDP[ti % 2], xdtwP[ti % 2], BtokP[ti % 2]
                        yb = banks[ti % 2]
                        if mode == "p":
                            for blk in range(3):
                                S.mm(b2[:, blk * 128:(blk + 1) * 128], Hst[:, blk, :], I32v("ident")[:, :], [Hst, cr["ident"]], [b2])
                            S.cp("act", HTb[:, :], b2[:, 0:384], [b2], [HTb])
                            S.mm(b2[:W, 0:384], Cs[:, c0:c0 + W], HTb[:, :], [Cs, HTb], [b2])
                        else:
                            S.tt("dve", Cm[:, :, :], Cs[:, c0:c0 + 128].unsqueeze(1).to_broadcast([128, NSEQ, 128]), cf["SMASK"][:, :, :], ALU.mult,
                                 [Cs, cf["SMASK"]], [Cm])
                            S.op("act", lambda e: e.activation(dtaexp[:W, :, :], GD32[:W, ti, gc0:gc0 + 6].unsqueeze(2).to_broadcast([W, 6, 64]), AF.Copy), [GD], [dtaexp])
                            for blk in range(3):
                                hb = 6 * g + 2 * blk
                                S.dma("sp", gstate, Hsm[:, :, :], ss_d.t[l, :, hb:hb + 2].rearrange("s h p n -> (h p) s n"), ss_d, Hsm)
                                for q4 in range(4):
                                    bk = banks[3 + q4]
                                    for s4 in range(4):
                                        s = q4 * 4 + s4
                                        S.mm(bk[:, s4 * 128:(s4 + 1) * 128], Hsm[:, s, :], I32v("ident")[:, :], [Hsm, cr["ident"]], [bk])
                                    S.cp("act", HTs[:, q4 * 4:(q4 + 1) * 4, :], bk[:, :].rearrange("p (s v) -> p s v", s=4), [bk], [HTs])
                                for s in range(NSEQ):
                                    S.mm(b2[:, blk * 128:(blk + 1) * 128], Cm[:, s, :], HTs[:, s, :], [Cm, HTs], [b2],
                                         start=(s == 0), stop=(s == NSEQ - 1))
                                S.tt("dve", xdtwm[:, :, :], xdtw_.t.bitcast(F32)[:, 2 * blk:2 * blk + 2, :].rearrange("p h q -> p (h q)").unsqueeze(1).to_broadcast([128, NSEQ, 128]),
                                     cf["SEQ_s"][:, :].unsqueeze(2).to_broadcast([128, NSEQ, 128]), ALU.mult, [xdtw_, cf["SEQ_s"]], [xdtwm])
                                S.mm(b7[:, 400 + blk * 16:416 + blk * 16], dtaexp[:, 2 * blk:2 * blk + 2, :].rearrange("p h q -> p (h q)"), cf["SEQ_s"][:, :],
                                     [dtaexp, cf["SEQ_s"]], [b7])
                                S.act(dchs[:, blk, :], b7[:, 400 + blk * 16:416 + blk * 16], AF.Exp, [b7], [dchs])
                                for q4 in range(4):
                                    bk = banks[3 + q4]
                                    for s4 in range(4):
                                        s = q4 * 4 + s4
                                        S.mm(bk[:, s4 * 128:(s4 + 1) * 128], xdtwm[:, s, :], Btok_[:, :], [xdtwm, Btok_], [bk])
                                    sl4 = slice(q4 * 4, (q4 + 1) * 4)
                                    S.tt("dve", Hsm[:, sl4, :], Hsm[:, sl4, :], dchs[:, blk, sl4].unsqueeze(2).to_broadcast([128, 4, 128]), ALU.mult,
                                         [Hsm, dchs], [Hsm])
                                    S.tt("dve", Hsm[:, sl4, :], Hsm[:, sl4, :], bk[:, :].rearrange("p (s v) -> p s v", s=4), ALU.add, [Hsm, bk], [Hsm])
                                S.dma("sp", gout, nss_d.t[l, :, 3 * g + blk].rearrange("s r n -> r s n"), Hsm[:, :, :], Hsm, nss_d)
                        S.tt("dve", yoff[:W, :, :], b2[:W, 0:384].rearrange("p (h q) -> p h q", h=6),
                             EX1[:W, ti, gc0:gc0 + 6].unsqueeze(2).to_broadcast([W, 6, 64]), ALU.mult, [b2, EX1], [yoff])
                        S.tt("dve", ytok[:W, :], yb[:W, 0:384], yoff[:W, :, :].rearrange("p h q -> p (h q)"), ALU.add, [yb, yoff], [ytok])
                        S.tt("dve", ytr[:W, :], ytok[:W, :], xD_[:W, :, :].rearrange("p h q -> p (h q)"), ALU.add, [ytok, xD_], [ytr])
                        if dbg_d is not None and l == 0 and ps_i == 0 and g == 0 and ti == DBG_TI:
                            dbg_dump(xdt_.t.bitcast(F32)[:, :, :].rearrange("p h q -> p (h q)"), xdt_, 256, 384)
                            dbg_dump(E6[:, 0, :], E6, 640, 128)
                            dbg_dump(MT6.t.bitcast(F32)[:, 0, :], MT6, 768, 128)
                            dbg_dump(ytok[:, :], ytok, 896, 384)
                            dbg_dump(Btok_.t.bitcast(F32)[:, :], Btok_, 1280, 128)
                            dbg_dump(yoff[:, :, :].rearrange("p h q -> p (h q)"), yoff, 1408, 384)
                            dbg_dump(ytr.t.bitcast(F32)[:, :], ytr, 1792, 384)
                        if mode == "p":
                            S.op("act", lambda e: e.activation(dtaexp[:W, :, :], GD32[:W, ti, gc0:gc0 + 6].unsqueeze(2).to_broadcast([W, 6, 64]), AF.Copy), [GD], [dtaexp])
                            for blk in range(3):
                                S.mm(b4[:, 256 + blk * 2:258 + blk * 2], dtaexp[:W, 2 * blk:2 * blk + 2, :].rearrange("p h q -> p (h q)"), cf["SEQ_p"][:W, 0:2],
                                     [dtaexp, cf["SEQ_p"]], [b4])
                            S.act(dch[:, :, :], b4[:, 256:262].rearrange("p (b t) -> p b t", b=3), AF.Exp, [b4], [dch])
                            for blk in range(3):
                                S.mm(b5[:, blk * 128:(blk + 1) * 128], xdtw_[:W, 2 * blk:2 * blk + 2, :].rearrange("p h q -> p (h q)"), Btok_[:W, :], [xdtw_, Btok_], [b5])
                            for blk in range(3):
                                S.stt("dve", Hst[:, blk, :], Hst[:, blk, :], dch[:, blk, 0:1], b5[:, blk * 128:(blk + 1) * 128], ALU.mult, ALU.add,
                                      [Hst, dch, b5], [Hst])
                        for blk in range(3):
                            S.mm(b6[:, blk * 128:blk * 128 + W], ytr[:W, blk * 128:(blk + 1) * 128], cr["ident"][:W, :W], [ytr, cr["ident"]], [b6])
                        S.tt("dve", yg[:, :, :W], b6[:, 0:384].rearrange("p (b i) -> p b i", b=3)[:, :, :W], zss[:, :, c0:c0 + W], ALU.mult, [b6, zss], [yg])
                        S.act(sqg[:, :, :W], yg[:, :, :W], AF.Square, [yg], [sqg])
                        for blk in range(3):
                            S.mm(b7[:, 384:384 + W], cr["ones"][:, :], sqg[:, blk, :W], [cr["ones"], sqg], [b7], start=(blk == 0), stop=(blk == 2))
                        rsq_from(rstd[:, :W], b7[:, 384:384 + W], 1.0 / 384, [b7], [rstd])
                        for blk in range(3):
                            S.stt("dve", ycat[:, 6 + 3 * g + blk, c0:c0 + W], yg[:, blk, :W], snw[:, 3 * g + blk:3 * g + blk + 1], rstd[:, :W],
                                  ALU.mult, ALU.mult, [yg, snw, rstd], [ycat])

                pre(0)
                for ti in range(NT):
                    if ti + 1 < NT:
                        pre(ti + 1)
                    post(ti)
                for blk in range(3):
                    if ps_i < last_pass:
                        S.dma("sp", gcar, car_s.t[l, 3 * g + blk], Hst[:, blk, :], Hst, car_s)
                    else:
                        S.dma("sp", gout, nsp_d.t[l, 3 * g + blk], Hst[:, blk, :], Hst, nsp_d)
            S.barrier()

    def pool_phase(l, ps_i, tiles, cgs, last_pass):
        NT = len(tiles)
        with contextlib.ExitStack() as ph:
            def A(name, shape, dt):
                return S.sb(name, shape, dt, stack=ph)
            pc = {k: A("p_" + k, CONST_SHAPES[k], F32R) for k in P_CONSTS}
            pw = A("pw", [128, 4, 128], F32)
            pwr = A("pwr", [128, 4, 128], F32R)
            utok = A("utok", [128, NTM, 128], F32R)
            utok32 = A("utok32", [128, NTM, 128], F32)
            uprev = A("uprev", [128, 128], F32)
            uprev_r = A("uprev_r", [128, 128], F32R)
            zp = A("zp", [128, TOKMAX], F32)
            uT = A("uT", [128, TOKMAX], F32R)
            pooledT = A("pooledT", [128, 128], F32R)
            hist = A("hist", [128, 2, 128], F32)
            hist_r = A("hist_r", [128, 2, 128], F32R)
            S.ms("pool", hist[:, :, :], 0.0, [hist])
            for k in P_CONSTS:
                n = CONST_SHAPES[k][1] * 128
                S.dma("sp", gmisc, scr8[:, 0:n], cd[k][:].rearrange("p a b -> p (a b)"), cd[k], scr8)
                S.cp("dve", pc[k][:].rearrange("p a b -> p (a b)"), scr8[:, 0:n], [scr8], [pc[k]])
            S.dma("sp", gmisc, pw[:], pw_d.t[l], pw_d, pw)
            S.cp("dve", pwr[:], pw[:], [pw], [pwr])
            for g in range(4):
                wbuf = load_block(win_ap(l, OFF_UP + g * 128), win_d)
                drain(proj_block(wbuf, cgs, lambda c0, n, b: S.cp("act", uT[:, c0:c0 + n], b[:, :n], [b], [uT])))
                for ti in range(NT):
                    c0, W, mode = tiles[ti]
                    b = dense_bank()
                    S.mm(b[:W, 0:128], uT[:, c0:c0 + W], cr["ident"][:, :], [uT, cr["ident"]], [b])
                    S.cp("act", utok32[:W, ti, :], b[:W, 0:128], [b], [utok32])
                    S.cp("dve", utok[:W, ti, :], b[:W, 0:128], [b], [utok])
                wbuf = load_block(win_ap(l, OFF_ZP + g * 128), win_d)
                drain(proj_block(wbuf, cgs, lambda c0, n, b: S.act(zp[:, c0:c0 + n], b[:, :n], AF.Silu, [b], [zp])))
                if ps_i > 0:
                    S.dma("sp", gcar, uprev[:, :], car_p.t[l, g], car_p, uprev)
                    S.cp("dve", uprev_r[:, :], uprev[:, :], [uprev], [uprev_r])
                for ti in range(NT):
                    c0, W, mode = tiles[ti]
                    bk = banks[3 + ti]
                    if mode == "p":
                        first = (ps_i == 0 and ti == 0)
                        band = pc["B0"] if first else pc["BC"]
                        S.mm(bk[:, 0:W], utok[:W, ti, :], band[:W, g, :W], [utok, band], [bk], start=True, stop=first)
                        if not first:
                            if ti == 0:
                                S.mm(bk[:, 0:W], uprev_r[:, :], pc["BP"][:, g, :W], [uprev_r, pc["BP"]], [bk], start=False, stop=True)
                            else:
                                S.mm(bk[:, 0:W], utok[:, ti - 1, :], pc["BP"][:, g, :W], [utok, pc["BP"]], [bk], start=False, stop=True)
                    else:
                        for half in range(2):
                            S.dma("sp", gstate, hist[:120, half, :],
                                  sp_d.t[l, half * 8:(half + 1) * 8, :, g * 128:(g + 1) * 128].rearrange("s r c -> (s r) c"), sp_d, hist)
                        S.cp("dve", hist_r[:, :, :], hist[:, :, :], [hist], [hist_r])
                        S.mm(bk[:, 0:W], utok[:W, ti, :], pc["BS"][:W, g, :W], [utok, pc["BS"]], [bk], start=True, stop=False)
                        S.mm(bk[:, 0:W], hist_r[:, 0, :], pc["BH"][:, g * 2, :W], [hist_r, pc["BH"]], [bk], start=False, stop=False)
                        S.mm(bk[:, 0:W], hist_r[:, 1, :], pc["BH"][:, g * 2 + 1, :W], [hist_r, pc["BH"]], [bk], start=False, stop=True)
                        S.dma("sp", gout, nps_d.t[l, :, 0:11, g * 128:(g + 1) * 128], sp_d.t[l, :, 4:15, g * 128:(g + 1) * 128], sp_d, nps_d)
                        for s in range(NSEQ):
                            S.dma("sp", gout, nps_d.t[l, s, 11:15, g * 128:(g + 1) * 128],
                                  utok32[s * DTK:(s + 1) * DTK, ti, :], utok32, nps_d)
                    S.cp("act", pooledT[:, :W], bk[:, 0:W], [bk], [pooledT])
                    S.mm(bk[:, 128:128 + W], pwr[:, g, :], pooledT[:, :W], [pwr, pooledT], [bk])
                    S.stt("dve", ycat[:, 12 + g, c0:c0 + W], bk[:, 128:128 + W], psc[:, g:g + 1], zp[:, c0:c0 + W], ALU.mult, ALU.mult,
                          [bk, psc, zp], [ycat])
                lt = NPT - 1
                if ps_i < last_pass:
                    S.dma("sp", gcar, car_p.t[l, g], utok32[:, lt, :], utok32, car_p)
                else:
                    S.dma("sp", gout, npp_d.t[l, :, g * 128:(g + 1) * 128], utok32[113:128, lt, :], utok32, npp_d)
            S.barrier()

    for ps_i in range(npass):
        last_pass = npass - 1
        tiles = [(t * 128, 128, "p") for t in range(NPT)]
        if ps_i == 0:
            tiles.append((PT, SWP, "s"))
        TOK = PT + (SWP if ps_i == 0 else 0)
        NT = len(tiles)
        cgs = [(0, PT)] + ([(PT, SWP)] if ps_i == 0 else [])
        S.dma("sp", gmisc, xT[:, :, 0:PT], xp_d.t.rearrange("(c p) t -> p c t", p=128)[:, :, ps_i * PT:(ps_i + 1) * PT], xp_d, xT)
        if ps_i == 0:
            S.dma("sp", gmisc, xT[:, :, PT:PT + SW], xs_d.t.rearrange("(c p) t -> p c t", p=128), xs_d, xT)
            S.ms("pool", xT[:, :, PT + SW:PT + SWP], 0.0, [xT])

        for l in range(depth):
            S.dma("sp", gmisc, normw[:], normw_d.t[l], normw_d, normw)
            S.dma("sp", gmisc, gcw[:], gcw_d.t[l], gcw_d, gcw)
            S.dma("sp", gmisc, scw[:], scw_d.t[l], scw_d, scw)
            S.dma("sp", gmisc, scb[:], scb_d.t[l], scb_d, scb)
            S.dma("sp", gmisc, gnw[:], gnw_d.t[l], gnw_d, gnw)
            S.dma("sp", gmisc, snw[:], snw_d.t[l], snw_d, snw)
            S.dma("sp", gmisc, psc[:], psc_d.t[l], psc_d, psc)
            for (o, n, srcd) in ((0, 6, galog_d), (6, 6, gdtb_d), (12, 12, salog_d), (24, 12, sdtb_d), (36, 12, sd_d)):
                S.dma("sp", gmisc, rowc[:, o:o + n], srcd.t[l:l + 1, :].to_broadcast([128, n]), srcd, rowc)
            S.act(negA[:, 0:6], rowc[:, 0:6], AF.Exp, [rowc], [negA])
            S.act(negA[:, 6:18], rowc[:, 12:24], AF.Exp, [rowc], [negA])
            S.ts("dve", negA[:], negA[:], -1.0, ALU.mult, [negA], [negA])
            S.dma("pool", gwsm, wsm[:, :, :], wsm_d.t[l], wsm_d, wsm)

            S.mark("L%d.%d params" % (ps_i, l))
            phase_a(tiles)
            S.mark("L%d.%d phaseA" % (ps_i, l))

            for ti, (c0, W, mode) in enumerate(tiles):
                b = banks[2]
                for c in range(16):
                    S.mm(b[:W, 0:24], hT[:, c, c0:c0 + W], wsm[:, c, :], [hT, wsm], [b], start=(c == 0), stop=(c == 15))
                S.cp("dve", small[:W, ti, :], b[:W, 0:24], [b], [small])
                sm = small[:W, ti, :]
                S.act(beta[:W, ti, :], sm[:, 0:6], AF.Exp, [small], [beta], scale=-1.0)
                S.ts("dve", beta[:W, ti, :], beta[:W, ti, :], 1.0, ALU.add, [beta], [beta])
                S.op("dve", lambda e: e.reciprocal(beta[:W, ti, :], beta[:W, ti, :]), [beta], [beta])
                S.ts("dve", nbeta[:W, ti, :], beta[:W, ti, :], -1.0, ALU.mult, [beta], [nbeta])
                S.tt("dve", tmp18[:W, 0:6], sm[:, 6:12], rowc[:W, 6:12], ALU.add, [small, rowc], [tmp18])
                S.tt("dve", tmp18[:W, 6:18], sm[:, 12:24], rowc[:W, 24:36], ALU.add, [small, rowc], [tmp18])
                S.act(tmp18b[:W, :], tmp18[:W, :], AF.Abs, [tmp18], [tmp18b])
                S.act(tmp18b[:W, :], tmp18b[:W, :], AF.Exp, [tmp18b], [tmp18b], scale=-1.0)
                S.act(tmp18b[:W, :], tmp18b[:W, :], AF.Ln, [tmp18b], [tmp18b], bias=onec[:W, 0:1])
                S.stt("dve", tmp18[:W, :], tmp18[:W, :], 0.0, tmp18b[:W, :], ALU.max, ALU.add, [tmp18, tmp18b], [tmp18])
                S.cp("dve", dtv[:W, ti, :], tmp18[:W, 6:18], [tmp18], [dtv])
                S.tt("dve", GD[:W, ti, :], tmp18[:W, :], negA[:W, :], ALU.mult, [tmp18, negA], [GD])
                S.ts("dve", negGD[:W, ti, :], GD.t.bitcast(F32)[:W, ti, :], -1.0, ALU.mult, [GD], [negGD])
                U = cr["U_p"] if mode == "p" else cr["U_s"]
                SM = cr["ones"] if mode == "p" else cr["SAME_s"]
                S.mm(b[:W, 32:50], U[:W, :W], GD[:W, ti, :], [U, GD], [b])
                S.mm(b[:W, 64:82], SM[:W, :W], GD[:W, ti, :], [SM, GD], [b])
                S.cp("dve", cum[:W, ti, :], b[:W, 32:50], [b], [cum])
                S.cp("dve", tot[:W, ti, :], b[:W, 64:82], [b], [tot])
                S.act(EX1[:W, ti, :], cum[:W, ti, :], AF.Exp, [cum], [EX1])
                S.act(EX3[:W, ti, :], tot[:W, ti, :], AF.Exp, [tot], [EX3])
                S.tt("dve", tmp18[:W, :], tot[:W, ti, :], cum[:W, ti, :], ALU.subtract, [tot, cum], [tmp18])
                S.act(EX2[:W, ti, :], tmp18[:W, :], AF.Exp, [tmp18], [EX2])
                if mode == "s":
                    S.tt("dve", gdm[:W, :, :], GD.t.bitcast(F32)[:W, ti, :].unsqueeze(1).to_broadcast([W, NSEQ, 18]),
                         cf["SEQ_s"][:W, :].unsqueeze(2).to_broadcast([W, NSEQ, 18]), ALU.mult, [GD, cf["SEQ_s"]], [gdm])
                    S.mm(b[:, 128:128 + NSEQ * 18], I32v("ones")[:W, :], gdm[:W, :, :].rearrange("p s h -> p (s h)"), [cr["ones"], gdm], [b])
                    S.act(glbc[:, :, :], b[:, 128:128 + NSEQ * 18].rearrange("p (s h) -> p s h", s=NSEQ), AF.Exp, [b], [glbc])

            if dbg_d is not None and l == 0 and ps_i == 0:
                for (o, n, bf) in ((0, 24, small), (24, 18, GD), (42, 18, cum), (60, 18, tot), (78, 18, EX1), (96, 18, EX2), (114, 12, dtv), (126, 6, beta)):
                    dbg_dump(bf.t.bitcast(F32)[:, DBG_TI, :] if bf is GD else bf[:, DBG_TI, :], bf, o, n)
            S.mark("L%d.%d small" % (ps_i, l))
            if "gdn" in parts:
                gdn_phase(l, ps_i, tiles, cgs, last_pass)
            else:
                S.ms("pool", ycat[:, 0:6, :], 0.0, [ycat])
            S.mark("L%d.%d gdn" % (ps_i, l))
            if "ssd" in parts:
                ssd_phase(l, ps_i, tiles, cgs, last_pass)
            else:
                S.ms("pool", ycat[:, 6:12, :], 0.0, [ycat])
            S.mark("L%d.%d ssd" % (ps_i, l))
            if "pool" in parts:
                pool_phase(l, ps_i, tiles, cgs, last_pass)
            else:
                S.ms("pool", ycat[:, 12:16, :], 0.0, [ycat])

            S.mark("L%d.%d pool" % (ps_i, l))
            for ob in range(16):
                wbuf = load_block(wout_ap(l, ob * 128), wout_d)
                for (c0, n) in cgs:
                    b = wide_bank()
                    for c in range(16):
                        S.mm(b[:, :n], wbuf[:, c, :], ycat[:, c, c0:c0 + n], [wbuf, ycat], [b], start=(c == 0), stop=(c == 15))
                    S.tt("dve", xT[:, ob, c0:c0 + n], xT[:, ob, c0:c0 + n], b[:, :n], ALU.add, [xT, b], [xT])

            S.mark("L%d.%d outproj" % (ps_i, l))
        final_out(tiles, ps_i)
        S.mark("L%d final" % ps_i)


_NC_CACHE = {}


def make_in_maps(inputs):
    consts = make_consts()
    maps = []
    f = lambda a: np.ascontiguousarray(np.asarray(a, dtype=np.float32))
    shared = {k: f(inputs[k]) for k in ("gdn_a_log", "gdn_dt_bias", "ssd_a_log", "ssd_dt_bias", "ssd_d")}
    w_in = np.asarray(inputs["w_in"], np.float32)
    wt = np.empty((DEPTH, NBLK, 128, 16, 128), np.float32)
    for b, c0 in enumerate(BLK_COLS):
        wt[:, b] = w_in[:, :, c0:c0 + 128].reshape(DEPTH, 16, 128, 128).transpose(0, 2, 1, 3)
    shared["w_in"] = wt
    wsmall = np.concatenate([w_in[:, :, OFF_B:OFF_B + 12], w_in[:, :, OFF_DT:OFF_DT + 12]], axis=2)
    shared["w_small"] = f(wsmall.reshape(DEPTH, 16, 128, 24).transpose(0, 2, 1, 3))
    w_out = np.asarray(inputs["w_out"], np.float32)
    shared["w_out"] = f(w_out.reshape(DEPTH, 16, 128, 16, 128).transpose(0, 3, 2, 1, 4))
    def pc(a, nb):
        a = np.asarray(a, np.float32)
        return f(a.reshape(a.shape[:-1] + (nb, 128)).swapaxes(-1, -2))
    shared["norm_w"] = pc(inputs["norm_w"], 16)
    shared["final_norm_w"] = pc(inputs["final_norm_w"], 16)
    shared["gdn_conv_w"] = f(np.asarray(inputs["gdn_conv_w"], np.float32).reshape(DEPTH, 4, 18, 128).transpose(0, 3, 2, 1))
    shared["ssd_conv_w"] = f(np.asarray(inputs["ssd_conv_w"], np.float32).reshape(DEPTH, 4, 10, 128).transpose(0, 3, 2, 1))
    shared["ssd_conv_b"] = pc(inputs["ssd_conv_b"], 10)
    shared["gdn_norm_w"] = f(np.asarray(inputs["gdn_norm_w"], np.float32).reshape(DEPTH, 128, 1))
    shared["ssd_norm_w"] = pc(inputs["ssd_norm_w"], 6)
    shared["pool_scale"] = pc(inputs["pool_scale"], 4)
    shared["pool_w"] = f(np.asarray(inputs["pool_w"], np.float32).transpose(0, 2, 1, 3))
    for k, v in consts.items():
        shared["c_" + k] = f(v)
    xp = np.asarray(inputs["x_prompt"], np.float32)
    xs = np.asarray(inputs["x_sample"], np.float32)
    for c in range(8):
        m = dict(shared)
        sl = slice(c * NSEQ, (c + 1) * NSEQ)
        m["x_prompt"] = f(xp[c % 4].T)
        m["x_sample"] = f(xs[sl].reshape(SW, D).T)
        m["state_gdn"] = f(np.asarray(inputs["state_gdn"])[:, sl])
        m["state_gdn_conv"] = f(np.asarray(inputs["state_gdn_conv"])[:, sl].reshape(DEPTH, NSEQ * 3, C_GDN))
        m["state_ssd"] = f(np.asarray(inputs["state_ssd"])[:, sl])
        m["state_ssd_conv"] = f(np.asarray(inputs["state_ssd_conv"])[:, sl].reshape(DEPTH, NSEQ * 3, C_SSD))
        m["state_pool"] = f(np.asarray(inputs["state_pool"])[:, sl])
        maps.append(m)
    return maps


def assemble(results):
    r = results
    y_prompt = np.stack([r[c]["y_prompt"].T for c in range(4)])
    y_sample = np.concatenate([r[c]["y_sample"].T.reshape(NSEQ, DTK, D) for c in range(8)], axis=0)
    ngp = np.stack([r[c]["new_gdn_p"] for c in range(4)], axis=1)
    ngcp = np.stack([r[c]["new_gdn_conv_p"] for c in range(4)], axis=1)
    nsp = np.stack([r[c]["new_ssd_p"].reshape(DEPTH, 12, 64, 128) for c in range(4)], axis=1)
    nscp = np.stack([r[c]["new_ssd_conv_p"] for c in range(4)], axis=1)
    npp = np.stack([r[c]["new_pool_p"] for c in range(4)], axis=1)
    ngs = np.concatenate([r[c]["new_gdn_s"] for c in range(8)], axis=1)
    ngcs = np.concatenate([r[c]["new_gdn_conv_s"].reshape(DEPTH, NSEQ, 3, C_GDN) for c in range(8)], axis=1)
    nss = np.concatenate([r[c]["new_ssd_s"].reshape(DEPTH, NSEQ, 12, 64, 128) for c in range(8)], axis=1)
    nscs = np.concatenate([r[c]["new_ssd_conv_s"].reshape(DEPTH, NSEQ, 3, C_SSD) for c in range(8)], axis=1)
    nps = np.concatenate([r[c]["new_pool_s"] for c in range(8)], axis=1)
    outs = (y_prompt, y_sample, ngp, ngcp, nsp, nscp, npp, ngs, ngcs, nss, nscs, nps)
    return tuple(np.ascontiguousarray(o.astype(np.float32)) for o in outs)


def kernel(**inputs):
    if "nc" not in _NC_CACHE:
        _NC_CACHE["nc"] = build()
    nc = _NC_CACHE["nc"]
    maps = make_in_maps(inputs)
    res = run_bass_kernel_spmd(nc, maps, core_ids=list(range(8)))
    return assemble(res.results)
```
